# Optimizing a Trainium2 kernel written in Bass

```python
import jax, jax.numpy as jnp
from jax import lax
import numpy as np

D_MODEL = 2048
BATCH = 8
SEQ = 2048
DEPTH = 4

GRID_W = 64
CTX_LEN = 256
HEAD_DIM = 128
D_BRANCH = D_MODEL // 2
N_Q_HEADS = D_BRANCH // HEAD_DIM
N_KV_HEADS = N_Q_HEADS // 4
Q_PER_KV = N_Q_HEADS // N_KV_HEADS
D_KV = N_KV_HEADS * HEAD_DIM
ROPE_THETA = 10000.0
ROPE_PAIRS_PER_AXIS = HEAD_DIM // 4
Q_BLOCK = 128
D_LRU = D_BRANCH
LRU_BLOCKS = 16
LRU_BLOCK_W = D_LRU // LRU_BLOCKS
LRU_C = 8.0
CONV_WIDTH = 4
CONV_LEFT = CONV_WIDTH // 2
D_FOURIER = D_BRANCH
FOURIER_GROUP = 128
N_FOURIER_GROUPS = D_FOURIER // FOURIER_GROUP
N_BRANCHES = 3
D_FF = -(-8 * D_MODEL // (3 * 256)) * 256
D_IN = D_BRANCH + 2 * D_KV + 2 * D_LRU + D_FOURIER
SPLITS = (D_BRANCH, D_BRANCH + D_KV, D_BRANCH + 2 * D_KV,
          D_BRANCH + 2 * D_KV + D_LRU, D_BRANCH + 2 * D_KV + 2 * D_LRU)
NORM_EPS = 1e-6

kernel_name = "hybrid_gated_gqa_rglru_fourier_dit"


def rmsnorm(x, g):
    xf = x.astype(jnp.float32)
    y = xf * lax.rsqrt(jnp.mean(xf * xf, axis=-1, keepdims=True) + NORM_EPS)
    return (y * g.astype(jnp.float32)).astype(x.dtype)


def axial_rope_tables(n):
    rows = n // GRID_W
    row = jnp.repeat(jnp.arange(rows, dtype=jnp.float32), GRID_W)
    col = jnp.tile(jnp.arange(GRID_W, dtype=jnp.float32), rows)
    inv = ROPE_THETA ** (-jnp.arange(ROPE_PAIRS_PER_AXIS, dtype=jnp.float32) / ROPE_PAIRS_PER_AXIS)
    ang = jnp.concatenate([row[:, None] * inv, col[:, None] * inv], axis=-1)
    return jnp.cos(ang), jnp.sin(ang)


def apply_rope(x, cos, sin):
    xf = x.astype(jnp.float32)
    x1, x2 = xf[..., :HEAD_DIM // 2], xf[..., HEAD_DIM // 2:]
    cb, sb = cos[None, :, None, :], sin[None, :, None, :]
    return jnp.concatenate([x1 * cb - x2 * sb, x2 * cb + x1 * sb], axis=-1).astype(x.dtype)


def attend(q, k, v):
    s = jnp.einsum('bqhgd,bkhd->bhgqk', q, k).astype(jnp.float32) * (HEAD_DIM ** -0.5)
    p = jax.nn.softmax(s, axis=-1).astype(v.dtype)
    return jnp.einsum('bhgqk,bkhd->bqhgd', p, v)


def attend_blocks(q, k, v):
    b, t = q.shape[:2]
    nb = t // Q_BLOCK
    qb = jnp.moveaxis(q.reshape(b, nb, Q_BLOCK, N_KV_HEADS, Q_PER_KV, HEAD_DIM), 1, 0)
    ob = lax.map(lambda qq: attend(qq, k, v), qb)
    return jnp.moveaxis(ob, 0, 1).reshape(b, t, D_BRANCH)


def short_conv(u, w, b):
    t = u.shape[1]
    up = jnp.pad(u, ((0, 0), (CONV_LEFT, CONV_WIDTH - 1 - CONV_LEFT), (0, 0)))
    y = b
    for j in range(CONV_WIDTH):
        y = y + up[:, j:j + t] * w[j]
    return y


def rglru_coeffs(u, w_r, b_r, w_i, b_i, lam):
    b, t, _ = u.shape
    ub = u.reshape(b, t, LRU_BLOCKS, LRU_BLOCK_W)
    r = jax.nn.sigmoid((jnp.einsum('btnc,ncd->btnd', ub, w_r).reshape(b, t, D_LRU) + b_r).astype(jnp.float32))
    i = jax.nn.sigmoid((jnp.einsum('btnc,ncd->btnd', ub, w_i).reshape(b, t, D_LRU) + b_i).astype(jnp.float32))
    log_a = -LRU_C * r * jax.nn.softplus(-lam.astype(jnp.float32))
    a = jnp.exp(log_a)
    bx = jnp.sqrt(-jnp.expm1(2.0 * log_a)) * i * u.astype(jnp.float32)
    return a, bx


def linear_scan(a, bx, h0, reverse):
    def step(h, ab):
        h = ab[0] * h + ab[1]
        return h, h
    h_last, hs = lax.scan(step, h0, (jnp.swapaxes(a, 0, 1), jnp.swapaxes(bx, 0, 1)), reverse=reverse)
    return h_last, jnp.swapaxes(hs, 0, 1)


def rglru_bidir(ul, uc, w_r, b_r, w_i, b_i, lam):
    yl = jnp.zeros(ul.shape, jnp.float32)
    yc = jnp.zeros(uc.shape, jnp.float32)
    for d, rev in enumerate((False, True)):
        ac, bc = rglru_coeffs(uc, w_r[d], b_r[d], w_i[d], b_i[d], lam[d])
        al, bl = rglru_coeffs(ul, w_r[d], b_r[d], w_i[d], b_i[d], lam[d])
        h0 = jnp.zeros((uc.shape[0], D_LRU), jnp.float32)
        hc_last, hc = linear_scan(ac, bc, h0, rev)
        _, hl = linear_scan(al, bl, hc_last, rev)
        yl = yl + hl
        yc = yc + hc
    return yl.astype(ul.dtype), yc.astype(uc.dtype)


def fourier_mix(u):
    b, t, _ = u.shape
    uf = u.astype(jnp.float32).reshape(b, t, N_FOURIER_GROUPS, FOURIER_GROUP)
    y = jnp.fft.fft2(uf, axes=(1, 3), norm='ortho').real
    return y.reshape(b, t, D_FOURIER).astype(u.dtype)


def gated_merge(h, y_att, y_rec, y_fou, w_branch, w_gate, b_gate, w_out):
    ys = jnp.stack([y_att, y_rec, y_fou], axis=2)
    br = jnp.einsum('btkc,kcd->btkd', ys, w_branch)
    g = jax.nn.sigmoid(h @ w_gate + b_gate).reshape(h.shape[0], h.shape[1], N_BRANCHES, D_MODEL)
    return jnp.sum(g * br, axis=2) @ w_out


def swiglu(h, w_in, w_out):
    gt, up = jnp.split(h @ w_in, 2, axis=-1)
    return (jax.nn.silu(gt) * up) @ w_out


def setup_inputs(seed: int = 0) -> dict:
    key = jax.random.key(seed)
    ks = jax.random.split(key, 26)
    f32 = jnp.float32

    def nrm(k, shape, scale):
        return scale * jax.random.normal(k, shape, f32)

    u = jax.random.uniform(ks[17], (DEPTH, 2, D_LRU), f32, minval=0.9, maxval=0.999)
    return {
        'x': nrm(ks[0], (BATCH, SEQ, D_MODEL), 1.0),
        'c': nrm(ks[1], (BATCH, D_MODEL), 1.0),
        'ctx': nrm(ks[2], (BATCH, CTX_LEN, D_MODEL), 1.0),
        'c_ctx': nrm(ks[3], (D_MODEL,), 1.0),
        'w_mod': nrm(ks[4], (DEPTH, D_MODEL, 6 * D_MODEL), D_MODEL ** -0.5),
        'b_mod': nrm(ks[5], (DEPTH, 6 * D_MODEL), 0.01),
        'g_norm1': 1.0 + nrm(ks[6], (DEPTH, D_MODEL), 0.02),
        'g_norm2': 1.0 + nrm(ks[7], (DEPTH, D_MODEL), 0.02),
        'w_in': nrm(ks[8], (DEPTH, D_MODEL, D_IN), D_MODEL ** -0.5),
        'q_gain': 1.0 + nrm(ks[9], (DEPTH, HEAD_DIM), 0.02),
        'k_gain': 1.0 + nrm(ks[10], (DEPTH, HEAD_DIM), 0.02),
        'conv_w': nrm(ks[11], (DEPTH, CONV_WIDTH, D_LRU), CONV_WIDTH ** -0.5),
        'conv_b': nrm(ks[12], (DEPTH, D_LRU), 0.01),
        'lru_w_r': nrm(ks[13], (DEPTH, 2, LRU_BLOCKS, LRU_BLOCK_W, LRU_BLOCK_W), LRU_BLOCK_W ** -0.5),
        'lru_b_r': nrm(ks[14], (DEPTH, 2, D_LRU), 0.01),
        'lru_w_i': nrm(ks[15], (DEPTH, 2, LRU_BLOCKS, LRU_BLOCK_W, LRU_BLOCK_W), LRU_BLOCK_W ** -0.5),
        'lru_b_i': nrm(ks[16], (DEPTH, 2, D_LRU), 0.01),
        'lru_lambda': jnp.log(u) - jnp.log1p(-u),
        'w_branch': nrm(ks[18], (DEPTH, N_BRANCHES, D_BRANCH, D_MODEL), D_BRANCH ** -0.5),
        'w_gate': nrm(ks[19], (DEPTH, D_MODEL, N_BRANCHES * D_MODEL), D_MODEL ** -0.5),
        'b_gate': nrm(ks[20], (DEPTH, N_BRANCHES * D_MODEL), 0.01),
        'w_out': nrm(ks[21], (DEPTH, D_MODEL, D_MODEL), D_MODEL ** -0.5),
        'w_ffn_in': nrm(ks[22], (DEPTH, D_MODEL, 2 * D_FF), D_MODEL ** -0.5),
        'w_ffn_out': nrm(ks[23], (DEPTH, D_FF, D_MODEL), D_FF ** -0.5),
        'g_final': 1.0 + nrm(ks[24], (D_MODEL,), 0.02),
    }


def reference(x, c, ctx, c_ctx, w_mod, b_mod, g_norm1, g_norm2, w_in, q_gain, k_gain,
              conv_w, conv_b, lru_w_r, lru_b_r, lru_w_i, lru_b_i, lru_lambda,
              w_branch, w_gate, b_gate, w_out, w_ffn_in, w_ffn_out, g_final):
    b, t = x.shape[0], x.shape[1]
    tc = ctx.shape[1]
    cos, sin = axial_rope_tables(t)
    xc = ctx
    for l in range(DEPTH):
        update_ctx = l < DEPTH - 1
        s1l, sc1l, g1l, s2l, sc2l, g2l = jnp.split(
            (jax.nn.silu(c) @ w_mod[l] + b_mod[l])[:, None, :], 6, axis=-1)
        s1c, sc1c, g1c, s2c, sc2c, g2c = jnp.split(
            jax.nn.silu(c_ctx) @ w_mod[l] + b_mod[l], 6, axis=-1)

        hl = rmsnorm(x, g_norm1[l]) * (1 + sc1l) + s1l
        hc = rmsnorm(xc, g_norm1[l]) * (1 + sc1c) + s1c
        ql, kl, vl, uxl, ugl, ufl = jnp.split(hl @ w_in[l], SPLITS, axis=-1)
        qc, kc, vc, uxc, ugc, ufc = jnp.split(hc @ w_in[l], SPLITS, axis=-1)

        kl = apply_rope(rmsnorm(kl.reshape(b, t, N_KV_HEADS, HEAD_DIM), k_gain[l]), cos, sin)
        kc = rmsnorm(kc.reshape(b, tc, N_KV_HEADS, HEAD_DIM), k_gain[l])
        vl = vl.reshape(b, t, N_KV_HEADS, HEAD_DIM)
        vc = vc.reshape(b, tc, N_KV_HEADS, HEAD_DIM)
        ql = apply_rope(rmsnorm(ql.reshape(b, t, N_Q_HEADS, HEAD_DIM), q_gain[l]), cos, sin)
        ql = ql.reshape(b, t, N_KV_HEADS, Q_PER_KV, HEAD_DIM)
        att_l = attend_blocks(ql, jnp.concatenate([kl, kc], axis=1), jnp.concatenate([vl, vc], axis=1))

        rec_l, rec_c = rglru_bidir(short_conv(uxl, conv_w[l], conv_b[l]),
                                   short_conv(uxc, conv_w[l], conv_b[l]),
                                   lru_w_r[l], lru_b_r[l], lru_w_i[l], lru_b_i[l], lru_lambda[l])
        rec_l = rec_l * jax.nn.gelu(ugl, approximate=True)

        fou_l = fourier_mix(ufl)

        x_mixed = x + g1l * gated_merge(hl, att_l, rec_l, fou_l, w_branch[l], w_gate[l], b_gate[l], w_out[l])
        if update_ctx:
            qc = rmsnorm(qc.reshape(b, tc, N_Q_HEADS, HEAD_DIM), q_gain[l])
            att_c = attend(qc.reshape(b, tc, N_KV_HEADS, Q_PER_KV, HEAD_DIM), kc, vc).reshape(b, tc, D_BRANCH)
            rec_c = rec_c * jax.nn.gelu(ugc, approximate=True)
            xc = xc + g1c * gated_merge(hc, att_c, rec_c, fourier_mix(ufc),
                                        w_branch[l], w_gate[l], b_gate[l], w_out[l])
        x = x_mixed

        x = x + g2l * swiglu(rmsnorm(x, g_norm2[l]) * (1 + sc2l) + s2l, w_ffn_in[l], w_ffn_out[l])
        if update_ctx:
            xc = xc + g2c * swiglu(rmsnorm(xc, g_norm2[l]) * (1 + sc2c) + s2c, w_ffn_in[l], w_ffn_out[l])
    return rmsnorm(x, g_final)
```

```python
import numpy as np
import ml_dtypes
import concourse.bass as bass
import concourse.mybir as mybir
from contextlib import ExitStack
from concourse.bass_utils import run_bass_kernel_spmd

F32 = mybir.dt.float32
BF16 = mybir.dt.bfloat16
AF = mybir.ActivationFunctionType
ALU = mybir.AluOpType

D = 2048
TL = 2048
TC = 256
NT = TL + TC
DEPTH = 4
DIN = 4608
DFF = 5632
NJ = 16
EPS = 1e-6
SEM_ROT = 30000

TBS = [(0, 512, 0), (512, 512, 0), (1024, 512, 0), (1536, 512, 0), (2048, 256, 1)]
HALF_BLOCKS_LAST = [[(0, 512, 0), (512, 512, 0)], [(1024, 512, 0), (1536, 512, 0)]]
HALF_BLOCKS = [[(0, 512, 0), (512, 512, 0), (1024, 128, 0)], [(1152, 512, 0), (1664, 384, 0), (2048, 256, 1)]]

C_BMOD = 0
C_GN1 = 96
C_GN2 = 112
C_BG = 128
C_CW = 176
C_CB = 208
C_BR = 216
C_BI = 232
C_LAM = 248
C_QG = 264
C_KG = 265
C_GF = 266
NCOL = 282


class Tok:
    __slots__ = ("sem", "val", "eng")

    def __init__(self, sem=None, val=None, eng=None):
        self.sem = sem
        self.val = val
        self.eng = eng


class T:
    __slots__ = ("name", "w", "r")

    def __init__(self, name):
        self.name = name
        self.w = None
        self.r = []


class Eng:
    def __init__(self, ctx, name, h):
        self.ctx = ctx
        self.name = name
        self.h = h
        self.sem = None
        self.cnt = 0
        self.nsem = 0
        self.waited = {}
        self.pending = []
        self.last = None
        self.n_ins = 0
        self.n_wait = 0

    def cur_sem(self):
        if self.sem is None or self.cnt >= SEM_ROT:
            self.sem = self.ctx.nc.alloc_semaphore(f"s_{self.name}_{self.nsem}")
            self.nsem += 1
            self.cnt = 0
        return self.sem


class Ctx:
    def __init__(self, nc, n_dma_sems=10):
        self.nc = nc
        self.pe = Eng(self, "pe", nc.tensor)
        self.act = Eng(self, "act", nc.scalar)
        self.dve = Eng(self, "dve", nc.vector)
        self.pool = Eng(self, "pool", nc.gpsimd)
        self.sp = Eng(self, "sp", nc.sync)
        self.engs = [self.pe, self.act, self.dve, self.pool, self.sp]
        self.dsems = {}
        for e in (self.sp, self.pool):
            self.dsems[e.name] = [[nc.alloc_semaphore(f"d_{e.name}_{i}"), 0, None] for i in range(n_dma_sems)]
        self.drr = {e.name: 0 for e in (self.sp, self.pool)}
        self.n_dma = 0

    def _wait(self, eng, tok):
        if tok is None:
            return
        if tok.eng is eng and eng is self.pe:
            return
        if tok.sem is None:
            raise RuntimeError(f"unresolved dependency needed by {eng.name}")
        key = tok.sem.name
        if eng.waited.get(key, 0) >= tok.val:
            return
        eng.h.wait_ge(tok.sem, tok.val)
        eng.waited[key] = tok.val
        eng.n_wait += 1

    def _deps(self, eng, reads, writes):
        for t in reads:
            self._wait(eng, t.w)
        for t in writes:
            self._wait(eng, t.w)
            for r in t.r:
                self._wait(eng, r)

    def _mark(self, tok, reads, writes):
        for t in reads:
            t.r = [r for r in t.r if not (r.sem is not None and tok.sem is not None and r.sem is tok.sem and r.val <= tok.val)]
            t.r.append(tok)
        for t in writes:
            t.w = tok
            t.r = []

    def op(self, eng, fn, reads=(), writes=(), sig=True):
        self._deps(eng, reads, writes)
        ins = fn()
        eng.n_ins += 1
        tok = Tok(eng=eng)
        if sig:
            sem = eng.cur_sem()
            ins.then_inc(sem, 1)
            eng.cnt += 1
            tok.sem = sem
            tok.val = eng.cnt
            for p in eng.pending:
                p.sem = sem
                p.val = eng.cnt
            eng.pending = []
            eng.last = tok
        else:
            eng.pending.append(tok)
        self._mark(tok, reads, writes)
        return tok

    def dma(self, q, out, in_, reads=(), writes=()):
        self._deps(q, reads, writes)
        lst = self.dsems[q.name]
        i = self.drr[q.name]
        self.drr[q.name] = (i + 1) % len(lst)
        slot = lst[i]
        if slot[2] is not None:
            self._wait(q, slot[2])
        ins = q.h.dma_start(out=out, in_=in_)
        slot[1] += 16
        ins.then_inc(slot[0], 16)
        tok = Tok(sem=slot[0], val=slot[1], eng=None)
        slot[2] = tok
        self.n_dma += 1
        self._mark(tok, reads, writes)
        return tok

    def barrier(self):
        toks = []
        for e in self.engs:
            assert not e.pending, f"pending nosig ops on {e.name} at barrier"
            if e.last is not None:
                toks.append(e.last)
        for lst in self.dsems.values():
            for slot in lst:
                if slot[2] is not None:
                    toks.append(slot[2])
        for e in self.engs:
            for t in toks:
                if t.eng is e and e is self.pe:
                    continue
                self._wait(e, t)

    def stats(self):
        d = {e.name: (e.n_ins, e.n_wait, e.nsem) for e in self.engs}
        d["dma"] = self.n_dma
        return d


class Ring:
    def __init__(self, aps, name):
        self.aps = aps
        self.ts = [T(f"{name}{i}") for i in range(len(aps))]
        self.i = 0

    def next(self):
        k = self.i % len(self.aps)
        self.i += 1
        return self.aps[k], self.ts[k]


class Builder:
    def __init__(self, n_layers=DEPTH, dbg=()):
        self.n_layers = n_layers
        self.dbg = set(dbg)
        nc = self.nc = bass.Bass("TRN2", target_bir_lowering=False)
        self.c = Ctx(nc)
        inp = lambda name, shape, dt=F32: nc.dram_tensor(name, list(shape), dt, kind="ExternalInput").ap()
        self.x = inp("x", [TL, D])
        self.ctx_in = inp("ctx", [TC, D])
        self.cvec = inp("cvec", [128, NJ, 2])
        self.cols = inp("cols", [DEPTH, 128, NCOL])
        self.w_mod = inp("w_mod", [DEPTH, D, 6 * D])
        self.w_in = inp("w_in", [DEPTH, D, DIN])
        self.lru_w_r = inp("lru_w_r", [DEPTH, 2, 16, 64, 64])
        self.lru_w_i = inp("lru_w_i", [DEPTH, 2, 16, 64, 64])
        self.w_branch = inp("w_branch", [DEPTH, NJ, 128, 3, 8, 128])
        self.w_gate = inp("w_gate", [DEPTH, D, 3 * D])
        self.w_out = inp("w_out", [DEPTH, NJ, 128, NJ, 128])
        self.w_ffn_in = inp("w_ffn_in", [DEPTH, D, 2 * DFF])
        self.w_ffn_out = inp("w_ffn_out", [DEPTH, NJ, 128, 44, 128])
        self.k_rope = inp("k_rope", [2, 128, TL], BF16)
        self.k_small = inp("k_small", [128, 128 * 2 + 256], BF16)
        self.k_ident32 = inp("k_ident32", [128, 128])
        self.k_dft = inp("k_dft", [2, TL, TL], BF16)
        self.k_dftc = inp("k_dftc", [2, TC, TC], BF16)
        self.out = nc.dram_tensor("out", [TL, D], F32, kind="ExternalOutput").ap()
        self.XT = self.scr("XT", [NJ, 128, NT], F32)
        self.QK = self.scr("QK", [10, 128, NT], BF16)
        self.VT = self.scr("VT", [18, 128, 256], BF16)
        self.U = self.scr("U", [24, 128, NT], BF16)
        self.G = self.scr("G", [48, 128, NT], BF16)
        self.YA = self.scr("YA", [8, 128, NT], BF16)
        self.YR = self.scr("YR", [8, 128, NT], BF16)
        self.YF = self.scr("YF", [8, 128, NT], BF16)
        self.MRG = self.scr("MRG", [NJ, 128, NT], BF16)
        self.HID = self.scr("HID", [44, 128, NT], BF16)
        self.ps = [nc.alloc_psum_tensor(f"ps{i}", [128, 512], F32).ap() for i in range(8)]
        self.pst = [T(f"ps{i}") for i in range(8)]
        sb = lambda name, shape, dt: nc.alloc_sbuf_tensor(name, list(shape), dt).ap()
        self.ident32 = sb("ident32", [128, 128], F32)
        self.ones32 = sb("ones32", [128, 128], F32)
        self.ksm = sb("ksm", [128, 512], BF16)
        self.ones_bf = sb("ones_bf", [128, 128], BF16)
        self.epsc = sb("epsc", [128, 1], F32)
        self.colsb = sb("colsb", [128, NCOL], F32)
        self.mod = sb("mod", [128, 96, 2], F32)
        self.A1 = sb("A1", [128, NJ, 2], F32)
        self.A2 = sb("A2", [128, NJ, 2], F32)
        self.nsp = sb("nsp", [128, 16], F32)
        self.nsp2 = sb("nsp2", [128, 16], F32)
        self.cs = sb("cs", [128, NJ, 2], BF16)
        self.cv32 = sb("cv32", [128, NJ, 2], F32)
        self.Tconst = T("const")
        self.Tcols = T("cols")
        self.Tmod = T("mod")

    def uniq(self, name):
        self._uid = getattr(self, "_uid", 0) + 1
        return f"{name}_{self._uid}"

    def scr(self, name, shape, dt):
        kind = "ExternalOutput" if name in self.dbg else "Internal"
        return self.nc.dram_tensor("scr_" + name, list(shape), dt, kind=kind).ap()

    def psring(self, idxs, name="psr"):
        r = Ring([self.ps[i] for i in idxs], name)
        r.ts = [self.pst[i] for i in idxs]
        return r

    def mm(self, out, lhsT, rhs, start, stop, reads, wt, sig=None):
        nc = self.nc
        if sig is None:
            sig = stop
        return self.c.op(self.c.pe, lambda: nc.tensor.matmul(out, lhsT=lhsT, rhs=rhs, start=start, stop=stop),
                         reads=reads, writes=[wt], sig=sig)

    def rstd_from_ps(self, ps_ap, ps_t, out_ap, out_t, scale):
        nc, c = self.nc, self.c
        c.op(c.act, lambda: nc.scalar.activation(out=out_ap, in_=ps_ap, func=AF.Sqrt, bias=self.epsc[:, 0:1], scale=scale),
             reads=[ps_t, self.Tconst], writes=[out_t])
        c.op(c.dve, lambda: nc.vector.reciprocal(out=out_ap, in_=out_ap), reads=[out_t], writes=[out_t])

    def phase_init(self):
        nc, c = self.nc, self.c
        with ExitStack() as es:
            sbt = lambda name, shape, dt: es.enter_context(nc.sbuf_tensor(self.uniq(name), list(shape), dt)).ap()
            c.dma(c.sp, self.ident32, self.k_ident32, writes=[self.Tconst])
            c.dma(c.sp, self.ksm, self.k_small, writes=[self.Tconst])
            c.op(c.dve, lambda: nc.vector.memset(self.ones32, 1.0), writes=[self.Tconst])
            c.op(c.dve, lambda: nc.vector.memset(self.ones_bf, 1.0), writes=[self.Tconst])
            c.op(c.dve, lambda: nc.vector.memset(self.epsc, EPS), writes=[self.Tconst])
            c.dma(c.sp, self.cv32, self.cvec, writes=[self.Tconst])
            c.op(c.act, lambda: nc.scalar.activation(out=self.cs, in_=self.cv32, func=AF.Silu), reads=[self.Tconst], writes=[self.Tconst])
            xin = Ring([sbt(f"xin{i}", [128, D], F32) for i in range(2)], "xin")
            xo = Ring([sbt(f"xo{i}", [128, NJ, 128], F32) for i in range(2)], "xo")
            psr = self.psring([0, 1, 2, 3])
            XTv = self.XT.rearrange("j p t -> p j t")
            k = 0
            for tt in range(18):
                src = self.x[tt * 128:(tt + 1) * 128, :] if tt < 16 else self.ctx_in[(tt - 16) * 128:(tt - 15) * 128, :]
                xa, xt = xin.next()
                c.dma(c.sp, xa, src, writes=[xt])
                oa, ot = xo.next()
                for j4 in range(4):
                    pa, pt = psr.next()
                    for jj in range(4):
                        j = j4 * 4 + jj
                        c.op(c.pe, lambda: nc.tensor.transpose(pa[:, jj * 128:(jj + 1) * 128], xa[:, j * 128:(j + 1) * 128], self.ident32),
                             reads=[xt, self.Tconst], writes=[pt], sig=(jj == 3))
                    dst = oa[:, j4 * 4:(j4 + 1) * 4, :]
                    src_ps = pa.rearrange("p (a b) -> p a b", b=128)
                    if k % 2 == 0:
                        c.op(c.act, lambda: nc.scalar.copy(out=dst, in_=src_ps), reads=[pt], writes=[ot])
                    else:
                        c.op(c.dve, lambda: nc.vector.tensor_copy(out=dst, in_=src_ps), reads=[pt], writes=[ot])
                    k += 1
                c.dma(c.sp, XTv[:, :, tt * 128:(tt + 1) * 128], oa, reads=[ot])
            c.barrier()

    def mod_emitter(self, l, es, gw=256):
        nc, c = self.nc, self.c
        sbt = lambda name, shape, dt: es.enter_context(nc.sbuf_tensor(self.uniq(name), list(shape), dt)).ap()
        f32r = Ring([sbt(f"wmf{i}", [128, NJ, gw], F32) for i in range(2)], "wmf")
        bfr = Ring([sbt(f"wmb{i}", [128, NJ, gw], BF16) for i in range(2)], "wmb")
        wv = self.w_mod[l].rearrange("(kc p) f -> p kc f", p=128)
        pm, pmt = self.ps[7], self.pst[7]
        stash = {}

        def emit_load(g):
            fa, ft_ = f32r.next()
            c.dma(c.sp, fa, wv[:, :, g * gw:(g + 1) * gw], writes=[ft_])
            ba, bt = bfr.next()
            c.op(c.act, lambda: nc.scalar.copy(out=ba, in_=fa), reads=[ft_], writes=[bt])
            stash[g] = (ba, bt)

        def emit_mm(g):
            ba, bt = stash.pop(g)
            nft = gw // 128
            for ft in range(nft):
                f = g * nft + ft
                for kc in range(NJ):
                    self.mm(pm[:, f * 2:(f + 1) * 2], ba[:, kc, ft * 128:(ft + 1) * 128], self.cs[:, kc, :],
                            kc == 0, kc == NJ - 1, [bt, self.Tconst], pmt, sig=(kc == NJ - 1 and ft == nft - 1))

        return emit_load, emit_mm

    def phase_mod(self, l, standalone):
        nc, c = self.nc, self.c
        with ExitStack() as es:
            c.dma(c.sp, self.colsb, self.cols[l], writes=[self.Tcols])
            pm, pmt = self.ps[7], self.pst[7]
            if standalone:
                emit_load, emit_mm = self.mod_emitter(l, es)
                for g in range(48):
                    emit_load(g)
                    if g > 0:
                        emit_mm(g - 1)
                emit_mm(47)
            pmv = pm[:, 0:192].rearrange("p (f v) -> p f v", v=2)
            cb = self.colsb
            for v in range(2):
                c.op(c.dve, lambda: nc.vector.tensor_tensor(out=self.mod[:, :, v], in0=pmv[:, :, v], in1=cb[:, C_BMOD:C_BMOD + 96], op=ALU.add),
                     reads=[pmt, self.Tcols], writes=[self.Tmod])
            for v in range(2):
                c.op(c.dve, lambda: nc.vector.scalar_tensor_tensor(out=self.A1[:, :, v], in0=self.mod[:, 16:32, v], scalar=1.0,
                                                                   in1=cb[:, C_GN1:C_GN1 + 16], op0=ALU.add, op1=ALU.mult),
                     reads=[self.Tmod, self.Tcols], writes=[self.Tmod])
                c.op(c.dve, lambda: nc.vector.scalar_tensor_tensor(out=self.A2[:, :, v], in0=self.mod[:, 64:80, v], scalar=1.0,
                                                                   in1=cb[:, C_GN2:C_GN2 + 16], op0=ALU.add, op1=ALU.mult),
                     reads=[self.Tmod, self.Tcols], writes=[self.Tmod])
            c.op(c.act, lambda: nc.scalar.activation(out=self.nsp, in_=cb[:, C_LAM:C_LAM + 16], func=AF.Exp, scale=-1.0),
                 reads=[self.Tcols], writes=[self.Tmod])
            c.op(c.act, lambda: nc.scalar.activation(out=self.nsp, in_=self.nsp, func=AF.Ln, bias=1.0, scale=1.0),
                 reads=[self.Tmod], writes=[self.Tmod])
            c.op(c.dve, lambda: nc.vector.tensor_scalar_mul(out=self.nsp2, in0=self.nsp, scalar1=-16.0), reads=[self.Tmod], writes=[self.Tmod])
            c.op(c.dve, lambda: nc.vector.tensor_scalar_mul(out=self.nsp, in0=self.nsp, scalar1=-8.0), reads=[self.Tmod], writes=[self.Tmod])
            c.barrier()

    def norm_to_h(self, es, A, Boff, h, hts, bw=512, nbuf=3, do_ctx=True):
        nc, c = self.nc, self.c
        sbt = lambda name, shape, dt: es.enter_context(nc.sbuf_tensor(self.uniq(name), list(shape), dt)).ap()
        xb_aps = [sbt(f"nxb{i}", [128, NJ, bw], F32) for i in range(nbuf)]
        xb_ts = [[T(f"nxb{i}_{j}") for j in range(NJ)] for i in range(nbuf)]
        sq = Ring([sbt("nsq0", [128, NJ, bw], BF16)], "nsq")
        rs = Ring([sbt(f"nrs{i}", [128, bw], F32) for i in range(2)], "nrs")
        psr = self.psring([5, 6])
        XTv = self.XT.rearrange("j p t -> p j t")
        stash = {}
        blocks = []
        for bi, (t0, n, v) in enumerate(TBS if do_ctx else TBS[:4]):
            for off in range(0, n, bw):
                blocks.append((t0 + off, min(bw, n - off), v, bi))

        def stage1(k):
            t0, n, v, bi = blocks[k]
            xa, xts = xb_aps[k % nbuf], xb_ts[k % nbuf]
            c.dma(c.sp, xa[:, :, :n], XTv[:, :, t0:t0 + n], writes=xts)
            sa, st = sq.next()
            c.op(c.act, lambda: nc.scalar.activation(out=sa[:, :, :n], in_=xa[:, :, :n], func=AF.Square), reads=xts, writes=[st])
            pa, pt = psr.next()
            for j in range(NJ):
                self.mm(pa[:, :n], self.ones_bf, sa[:, j, :n], j == 0, j == NJ - 1, [st, self.Tconst], pt)
            stash[k] = (xa, xts, pa, pt)

        def stage2(k):
            t0, n, v, bi = blocks[k]
            xa, xts, pa, pt = stash.pop(k)
            ra, rt = rs.next()
            self.rstd_from_ps(pa[:, :n], pt, ra[:, :n], rt, 1.0 / D)
            for j in range(NJ):
                c.op(c.dve, lambda: nc.vector.scalar_tensor_tensor(out=xa[:, j, :n], in0=xa[:, j, :n], scalar=A[:, j, v:v + 1], in1=ra[:, :n],
                                                                   op0=ALU.mult, op1=ALU.mult), reads=[xts[j], rt, self.Tmod], writes=[xts[j]])
                c.op(c.act, lambda: nc.scalar.activation(out=h[:, j, t0:t0 + n], in_=xa[:, j, :n], func=AF.Identity,
                                                         bias=self.mod[:, Boff + j, v:v + 1], scale=1.0),
                     reads=[xts[j], self.Tmod], writes=[hts[bi][j]])

        stage1(0)
        for k in range(len(blocks)):
            if k + 1 < len(blocks):
                stage1(k + 1)
            stage2(k)

    def phase_win(self, l):
        nc, c = self.nc, self.c
        with ExitStack() as es:
            sbt = lambda name, shape, dt: es.enter_context(nc.sbuf_tensor(self.uniq(name), list(shape), dt)).ap()
            h = sbt("h", [128, NJ, NT], BF16)
            hts = [[T(f"h{i}_{j}") for j in range(NJ)] for i in range(5)]
            with ExitStack() as es2:
                self.norm_to_h(es2, self.A1, 0, h, hts)
                c.barrier()
            wr = Ring([sbt(f"ww{i}", [128, NJ, 512], BF16) for i in range(3)], "ww")
            stg = Ring([sbt(f"stg{i}", [128, NT], BF16) for i in range(3)], "stg")
            qf = Ring([sbt(f"qf{i}", [128, 512], F32) for i in range(4)], "qf")
            sqq = Ring([sbt(f"sqq{i}", [128, 512], BF16) for i in range(4)], "sqq")
            rsq = Ring([sbt(f"rsq{i}", [128, 512], F32) for i in range(4)], "rsq")
            qn = Ring([sbt(f"qn{i}", [128, 512], BF16) for i in range(4)], "qn")
            t1r = Ring([sbt(f"t1r{i}", [128, 512], F32) for i in range(2)], "t1r")
            t2r = Ring([sbt(f"t2r{i}", [128, 512], F32) for i in range(2)], "t2r")
            vtr = Ring([sbt(f"vtr{i}", [128, 256], BF16) for i in range(3)], "vtr")
            rope = sbt("rope", [128, 2, TL], BF16)
            Trope = T("rope")
            c.dma(c.sp, rope, self.k_rope.rearrange("a p t -> p a t"), writes=[Trope])
            psm = self.psring([0, 1, 2, 3])
            ps2 = self.psring([4, 5])
            ps3 = self.psring([6, 7])
            rotT = self.ksm[:, 128:256]
            cb = self.colsb
            wv = self.w_in[l].rearrange("(kc p) f -> p kc f", p=128)
            wgs = []
            for grp in range(3):
                wa, wt = wr.next()
                c.dma(c.pool, wa, wv[:, :, grp * 512:(grp + 1) * 512], writes=[wt])
                wgs.append((wa, wt))
            items = [(f, bi) for f in range(10) for bi in range(5)]
            stA, stB, fstage, fdone = {}, {}, {}, {}

            def finish_block(f):
                fdone[f] = fdone.get(f, 0) + 1
                if fdone[f] == 5:
                    sa, st = fstage.pop(f)
                    c.dma(c.sp, self.QK[f], sa, reads=[st])

            def stageA(i):
                f, bi = items[i]
                t0, n, v = TBS[bi]
                wa, wt = wgs[f // 4]
                ft = f % 4
                if f not in fstage:
                    fstage[f] = stg.next()
                pa, pt = psm.next()
                for kc in range(NJ):
                    self.mm(pa[:, :n], wa[:, kc, ft * 128:(ft + 1) * 128], h[:, kc, t0:t0 + n], kc == 0, kc == NJ - 1, [hts[bi][kc], wt], pt)
                qa, qt = qf.next()
                c.op(c.act, lambda: nc.scalar.copy(out=qa[:, :n], in_=pa[:, :n]), reads=[pt], writes=[qt])
                s2, s2t = sqq.next()
                c.op(c.act, lambda: nc.scalar.activation(out=s2[:, :n], in_=pa[:, :n], func=AF.Square), reads=[pt], writes=[s2t])
                stA[i] = (qa, qt, s2, s2t)

            def stageB(i):
                f, bi = items[i]
                t0, n, v = TBS[bi]
                qa, qt, s2, s2t = stA.pop(i)
                sa, st = fstage[f]
                gcol = cb[:, C_QG:C_QG + 1] if f < 8 else cb[:, C_KG:C_KG + 1]
                p2, p2t = ps2.next()
                self.mm(p2[:, :n], self.ones_bf, s2[:, :n], True, True, [s2t, self.Tconst], p2t)
                ra, rt = rsq.next()
                self.rstd_from_ps(p2[:, :n], p2t, ra[:, :n], rt, 1.0 / 128)
                if v == 1:
                    c.op(c.dve, lambda: nc.vector.scalar_tensor_tensor(out=sa[:, t0:t0 + n], in0=qa[:, :n], scalar=gcol, in1=ra[:, :n],
                                                                       op0=ALU.mult, op1=ALU.mult), reads=[qt, rt, self.Tcols], writes=[st])
                    finish_block(f)
                else:
                    na, nt_ = qn.next()
                    c.op(c.dve, lambda: nc.vector.scalar_tensor_tensor(out=na[:, :n], in0=qa[:, :n], scalar=gcol, in1=ra[:, :n],
                                                                       op0=ALU.mult, op1=ALU.mult), reads=[qt, rt, self.Tcols], writes=[nt_])
                    stB[i] = (na, nt_)

            def stageC(i):
                f, bi = items[i]
                t0, n, v = TBS[bi]
                if v == 1:
                    return
                na, nt_ = stB.pop(i)
                sa, st = fstage[f]
                p3, p3t = ps3.next()
                self.mm(p3[:, :n], rotT, na[:, :n], True, True, [nt_, self.Tconst], p3t)
                a1, a1t = t1r.next()
                c.op(c.pool, lambda: nc.gpsimd.tensor_tensor(out=a1[:, :n], in0=na[:, :n], in1=rope[:, 0, t0:t0 + n], op=ALU.mult),
                     reads=[nt_, Trope], writes=[a1t])
                a2, a2t = t2r.next()
                c.op(c.dve, lambda: nc.vector.tensor_tensor(out=a2[:, :n], in0=p3[:, :n], in1=rope[:, 1, t0:t0 + n], op=ALU.mult),
                     reads=[p3t, Trope], writes=[a2t])
                c.op(c.pool, lambda: nc.gpsimd.tensor_tensor(out=sa[:, t0:t0 + n], in0=a1[:, :n], in1=a2[:, :n], op=ALU.add),
                     reads=[a1t, a2t], writes=[st])
                finish_block(f)

            NI = len(items)
            for i in range(NI + 2):
                if i < NI:
                    stageA(i)
                if 0 <= i - 1 < NI:
                    stageB(i - 1)
                if 0 <= i - 2 < NI:
                    stageC(i - 2)
            wa, wt = wgs[2]
            for tt in range(18):
                pa, pt = psm.next()
                for kc in range(NJ):
                    self.mm(pa[:, 0:256], h[:, kc, tt * 128:(tt + 1) * 128], wa[:, kc, 256:512], kc == 0, kc == NJ - 1,
                            [hts[min(tt // 4, 4)][kc], wt], pt)
                va, vt = vtr.next()
                c.op(c.act, lambda: nc.scalar.copy(out=va, in_=pa[:, 0:256]), reads=[pt], writes=[vt])
                c.dma(c.sp, self.VT[tt], va, reads=[vt])
            ke = 0
            for grp in range(3, 9):
                wa, wt = wr.next()
                c.dma(c.pool, wa, wv[:, :, grp * 512:(grp + 1) * 512], writes=[wt])
                for ft in range(4):
                    f = grp * 4 + ft
                    sa, st = stg.next()
                    for bi, (t0, n, v) in enumerate(TBS):
                        pa, pt = psm.next()
                        for kc in range(NJ):
                            self.mm(pa[:, :n], wa[:, kc, ft * 128:(ft + 1) * 128], h[:, kc, t0:t0 + n], kc == 0, kc == NJ - 1, [hts[bi][kc], wt], pt)
                        if ke % 2 == 0:
                            c.op(c.act, lambda: nc.scalar.copy(out=sa[:, t0:t0 + n], in_=pa[:, :n]), reads=[pt], writes=[st])
                        else:
                            c.op(c.dve, lambda: nc.vector.tensor_copy(out=sa[:, t0:t0 + n], in_=pa[:, :n]), reads=[pt], writes=[st])
                        ke += 1
                    c.dma(c.sp, self.U[f - 12], sa, reads=[st])
            wg = self.w_gate[l].rearrange("(kc p) f -> p kc f", p=128)
            for grp in range(12):
                wa, wt = wr.next()
                c.dma(c.pool, wa, wg[:, :, grp * 512:(grp + 1) * 512], writes=[wt])
                for ft in range(4):
                    f = grp * 4 + ft
                    sa, st = stg.next()
                    for bi, (t0, n, v) in enumerate(TBS):
                        pa, pt = psm.next()
                        for kc in range(NJ):
                            self.mm(pa[:, :n], wa[:, kc, ft * 128:(ft + 1) * 128], h[:, kc, t0:t0 + n], kc == 0, kc == NJ - 1, [hts[bi][kc], wt], pt)
                        c.op(c.act, lambda: nc.scalar.activation(out=sa[:, t0:t0 + n], in_=pa[:, :n], func=AF.Sigmoid,
                                                                 bias=cb[:, C_BG + f:C_BG + f + 1], scale=1.0), reads=[pt, self.Tcols], writes=[st])
                    c.dma(c.sp, self.G[f], sa, reads=[st])
            c.barrier()

    def phase_attn(self, l, do_ctx=True):
        nc, c = self.nc, self.c
        with ExitStack() as es:
            sbt = lambda name, shape, dt: es.enter_context(nc.sbuf_tensor(self.uniq(name), list(shape), dt)).ap()
            q = sbt("q", [128, 8, NT], BF16)
            k = sbt("k", [128, 2, NT], BF16)
            vt = sbt("vt", [128, 18, 256], BF16)
            Tqs, Tk, Tv = [T(f"q{i}") for i in range(8)], T("k"), T("v")
            QKv = self.QK.rearrange("f p t -> p f t")
            c.dma(c.sp, k, QKv[:, 8:10, :], writes=[Tk])
            c.dma(c.sp, q[:, 0, :], self.QK[0], writes=[Tqs[0]])
            c.dma(c.sp, vt, self.VT.rearrange("a p e -> p a e"), writes=[Tv])
            for i in range(1, 8):
                c.dma(c.sp, q[:, i, :], self.QK[i], writes=[Tqs[i]])
            ptr = Ring([sbt(f"pt{i}", [128, 512], BF16) for i in range(4)], "pt")
            rl = Ring([sbt(f"rl{i}", [128, 512], F32) for i in range(2)], "rl")
            stg = Ring([sbt(f"ostg{i}", [128, 512], BF16) for i in range(3)], "ostg")
            pss = self.psring([0, 1, 2, 3])
            pso = self.psring([4, 5])
            psl = self.psring([6, 7])
            scale = 1.0 / np.sqrt(128.0)

            def attend(hd, q0, nq, kts):
                g = hd // 4
                oa, ot = pso.next()
                la, lt = psl.next()
                nk = len(kts)
                S = []

                def emit_s(i):
                    kt = kts[i]
                    sa, st = pss.next()
                    self.mm(sa[:, :nq], k[:, g, kt * 128:(kt + 1) * 128], q[:, hd, q0:q0 + nq], True, True, [Tk, Tqs[hd]], st)
                    S.append((sa, st))

                emit_s(0)
                if nk > 1:
                    emit_s(1)
                for i, kt in enumerate(kts):
                    sa, st = S[i]
                    pa, pt_ = ptr.next()
                    c.op(c.act, lambda: nc.scalar.activation(out=pa[:, :nq], in_=sa[:, :nq], func=AF.Exp, scale=scale), reads=[st], writes=[pt_])
                    if i + 2 < nk:
                        emit_s(i + 2)
                    self.mm(oa[:, :nq], vt[:, kt, g * 128:(g + 1) * 128], pa[:, :nq], i == 0, i == nk - 1, [Tv, pt_], ot)
                    self.mm(la[:, :nq], self.ones_bf, pa[:, :nq], i == 0, i == nk - 1, [self.Tconst, pt_], lt, sig=True)
                ra, rt = rl.next()
                c.op(c.dve, lambda: nc.vector.reciprocal(out=ra[:, :nq], in_=la[:, :nq]), reads=[lt], writes=[rt])
                ga, gt = stg.next()
                c.op(c.dve, lambda: nc.vector.tensor_tensor(out=ga[:, :nq], in0=oa[:, :nq], in1=ra[:, :nq], op=ALU.mult), reads=[ot, rt], writes=[gt])
                c.dma(c.sp, self.YA[hd][:, q0:q0 + nq], ga[:, :nq], reads=[gt])

            for hd in range(8):
                for qb in range(4):
                    attend(hd, qb * 512, 512, list(range(18)))
                if do_ctx:
                    attend(hd, TL, TC, [16, 17])
            c.barrier()

    def phase_lru(self, l):
        nc, c = self.nc, self.c
        with ExitStack() as es:
            sbt = lambda name, shape, dt: es.enter_context(nc.sbuf_tensor(self.uniq(name), list(shape), dt)).ap()
            cb = self.colsb
            bd = sbt("bd", [128, 2, 2, 8, 128], BF16)
            Tbd = T("bd")
            c.op(c.pool, lambda: nc.gpsimd.memset(bd, 0.0), writes=[Tbd])
            for d in range(2):
                for ri, w in enumerate((self.lru_w_r, self.lru_w_i)):
                    wv = w[l, d].rearrange("(j two) c e -> two c j e", two=2)
                    c.dma(c.pool, bd[0:64, d, ri, :, 0:64], wv[0], writes=[Tbd])
                    c.dma(c.pool, bd[64:128, d, ri, :, 64:128], wv[1], writes=[Tbd])
            mk = lambda name, dt, k=1, w=NT: Ring([sbt(f"{name}{i}", [128, w], dt) for i in range(k)], name)
            ur = mk("lu", BF16, 3)
            ugr = mk("lug", BF16, 3)
            vr = mk("lv", F32, 2)
            vbr = mk("lvb", BF16, 2)
            rr = mk("lr", F32, 2)
            ir = mk("li", F32, 2)
            ar = mk("la", F32, 2)
            er = mk("le", F32, 2)
            hr = mk("lh", F32, 2)
            yr = mk("ly", F32, 2)
            gr = mk("lg", F32, 1)
            sr = mk("lst", BF16, 2)
            psr = self.psring([0, 1, 2, 3, 4, 5, 6, 7])
            segs = [(0, TL), (TL, TC)]

            def rev(ap, t0, n):
                a = ap[:, t0:t0 + n]
                pstep = a.ap[0][0]
                return bass.AP(a.tensor, a.offset + (n - 1), [[pstep, 128], [-1, n]])

            def load(j):
                ua, ut = ur.next()
                c.dma(c.sp, ua, self.U[j], writes=[ut])
                uga, ugt = ugr.next()
                c.dma(c.sp, uga, self.U[8 + j], writes=[ugt])
                return ua, ut, uga, ugt

            def prep(j, ld):
                ua, ut, uga, ugt = ld
                va, vt_ = vr.next()
                wcol = lambda tap: cb[:, C_CW + tap * 8 + j:C_CW + tap * 8 + j + 1]
                c.op(c.dve, lambda: nc.vector.tensor_scalar(out=va, in0=ua, scalar1=wcol(2), scalar2=cb[:, C_CB + j:C_CB + j + 1],
                                                            op0=ALU.mult, op1=ALU.add), reads=[ut, self.Tcols], writes=[vt_])
                for (s0, sn) in segs:
                    for tap, sh in ((0, -2), (1, -1), (3, 1)):
                        lo = max(0, -sh)
                        hi = sn - max(0, sh)
                        c.op(c.dve, lambda: nc.vector.scalar_tensor_tensor(out=va[:, s0 + lo:s0 + hi], in0=ua[:, s0 + lo + sh:s0 + hi + sh], scalar=wcol(tap),
                                                                           in1=va[:, s0 + lo:s0 + hi], op0=ALU.mult, op1=ALU.add),
                             reads=[ut, vt_, self.Tcols], writes=[vt_])
                vba, vbt = vbr.next()
                c.op(c.pool, lambda: nc.gpsimd.tensor_copy(out=vba, in_=va), reads=[vt_], writes=[vbt])
                return va, vt_, vba, vbt, uga, ugt

            def stageS(j, d, pr):
                va, vt_, vba, vbt, uga, ugt = pr
                col = d * 8 + j
                ra, rt = rr.next()
                ia, it = ir.next()
                for (t0, n, v) in TBS:
                    pa, pt = psr.next()
                    self.mm(pa[:, :n], bd[:, d, 0, j, :], vba[:, t0:t0 + n], True, True, [Tbd, vbt], pt)
                    c.op(c.act, lambda: nc.scalar.activation(out=ra[:, t0:t0 + n], in_=pa[:, :n], func=AF.Sigmoid,
                                                             bias=cb[:, C_BR + col:C_BR + col + 1], scale=1.0), reads=[pt, self.Tcols], writes=[rt])
                    pa, pt = psr.next()
                    self.mm(pa[:, :n], bd[:, d, 1, j, :], vba[:, t0:t0 + n], True, True, [Tbd, vbt], pt)
                    c.op(c.act, lambda: nc.scalar.activation(out=ia[:, t0:t0 + n], in_=pa[:, :n], func=AF.Sigmoid,
                                                             bias=cb[:, C_BI + col:C_BI + col + 1], scale=1.0), reads=[pt, self.Tcols], writes=[it])
                return ra, rt, ia, it

            def stageE(j, d, pr, sres):
                va, vt_, vba, vbt, uga, ugt = pr
                ra, rt, ia, it = sres
                col = d * 8 + j
                aa, at = ar.next()
                ea, et = er.next()
                c.op(c.act, lambda: nc.scalar.activation(out=aa, in_=ra, func=AF.Exp, scale=self.nsp[:, col:col + 1]), reads=[rt, self.Tmod], writes=[at])
                c.op(c.pool, lambda: nc.gpsimd.tensor_tensor(out=ia, in0=ia, in1=va, op=ALU.mult), reads=[it, vt_], writes=[it])
                c.op(c.pool, lambda: nc.gpsimd.tensor_tensor(out=ea, in0=aa, in1=aa, op=ALU.mult), reads=[at], writes=[et])
                return aa, at, ea, et, ia, it

            def stageY(j, d, eres, ya, yt):
                aa, at, ea, et, ia, it = eres
                c.op(c.act, lambda: nc.scalar.activation(out=ea, in_=ea, func=AF.Sqrt, bias=1.0, scale=-1.0), reads=[et], writes=[et])
                c.op(c.dve, lambda: nc.vector.tensor_tensor(out=ia, in0=ia, in1=ea, op=ALU.mult), reads=[it, et], writes=[it])
                if d == 0:
                    ha, ht = ya, yt
                    c.op(c.dve, lambda: nc.vector.tensor_tensor_scan(out=ha[:, TL:NT], data0=aa[:, TL:NT], data1=ia[:, TL:NT], initial=0.0,
                                                                     op0=ALU.mult, op1=ALU.add), reads=[at, it], writes=[ht])
                    c.op(c.dve, lambda: nc.vector.tensor_tensor_scan(out=ha[:, 0:TL], data0=aa[:, 0:TL], data1=ia[:, 0:TL], initial=ha[:, NT - 1:NT],
                                                                     op0=ALU.mult, op1=ALU.add), reads=[at, it, ht], writes=[ht])
                else:
                    ha, ht = hr.next()
                    c.op(c.dve, lambda: nc.vector.tensor_tensor_scan(out=rev(ha, TL, TC), data0=rev(aa, TL, TC), data1=rev(ia, TL, TC), initial=0.0,
                                                                     op0=ALU.mult, op1=ALU.add), reads=[at, it], writes=[ht])
                    c.op(c.dve, lambda: nc.vector.tensor_tensor_scan(out=rev(ha, 0, TL), data0=rev(aa, 0, TL), data1=rev(ia, 0, TL), initial=ha[:, TL:TL + 1],
                                                                     op0=ALU.mult, op1=ALU.add), reads=[at, it, ht], writes=[ht])
                    c.op(c.pool, lambda: nc.gpsimd.tensor_tensor(out=ya, in0=ya, in1=ha, op=ALU.add), reads=[yt, ht], writes=[yt])

            def tail(j, pr, ya, yt):
                va, vt_, vba, vbt, uga, ugt = pr
                ga, gt = gr.next()
                c.op(c.act, lambda: nc.scalar.activation(out=ga, in_=uga, func=AF.Gelu_apprx_tanh), reads=[ugt], writes=[gt])
                sa, st = sr.next()
                c.op(c.dve, lambda: nc.vector.tensor_tensor(out=sa, in0=ya, in1=ga, op=ALU.mult), reads=[yt, gt], writes=[st])
                c.dma(c.sp, self.YR[j], sa, reads=[st])

            lds = {0: load(0), 1: load(1)}
            pr = prep(0, lds.pop(0))
            for j in range(8):
                if j + 2 < 8:
                    lds[j + 2] = load(j + 2)
                ya, yt = yr.next()
                s0 = stageS(j, 0, pr)
                s1 = stageS(j, 1, pr)
                e0 = stageE(j, 0, pr, s0)
                e1 = stageE(j, 1, pr, s1)
                stageY(j, 0, e0, ya, yt)
                pr_next = prep(j + 1, lds.pop(j + 1)) if j + 1 < 8 else None
                stageY(j, 1, e1, ya, yt)
                tail(j, pr, ya, yt)
                pr = pr_next
            c.barrier()

    def phase_fourier(self, l, do_ctx=True):
        nc, c = self.nc, self.c
        with ExitStack() as es:
            sbt = lambda name, shape, dt: es.enter_context(nc.sbuf_tensor(self.uniq(name), list(shape), dt)).ap()
            A = sbt("fA", [128, 8, 18, 256], BF16)
            At = [T(f"fA{g}") for g in range(8)]
            ufr = Ring([sbt(f"fu{i}", [128, NT], BF16) for i in range(2)], "fu")
            tabr = Ring([sbt(f"ftab{i}", [128, 2, 16, 512], BF16) for i in range(2)], "ftab")
            tabc = sbt("ftabc", [128, 2, 2, 256], BF16)
            Ttc = T("ftabc")
            stg = Ring([sbt(f"fstg{i}", [128, 512], BF16) for i in range(3)], "fstg")
            c.dma(c.sp, tabc, self.k_dftc.rearrange("a (tt p) u -> p a tt u", p=128), writes=[Ttc])
            csG = self.ksm[:, 256:512]
            ps1 = self.psring([0, 1, 2, 3])
            ps2 = self.psring([4, 5, 6, 7])
            ke = 0
            def preu(g):
                ua, ut = ufr.next()
                c.dma(c.sp, ua, self.U[16 + g], writes=[ut])
                return ua, ut

            nxtu = preu(0)
            for g in range(8):
                ua, ut = nxtu
                if g + 1 < 8:
                    nxtu = preu(g + 1)
                for tt in range(18):
                    pa, pt = ps1.next()
                    self.mm(pa[:, 0:256], ua[:, tt * 128:(tt + 1) * 128], csG, True, True, [ut, self.Tconst], pt)
                    if ke % 2 == 0:
                        c.op(c.act, lambda: nc.scalar.copy(out=A[:, g, tt, :], in_=pa[:, 0:256]), reads=[pt], writes=[At[g]])
                    else:
                        c.op(c.dve, lambda: nc.vector.tensor_copy(out=A[:, g, tt, :], in_=pa[:, 0:256]), reads=[pt], writes=[At[g]])
                    ke += 1
            dv = self.k_dft.rearrange("a (tt p) u -> p a tt u", p=128)
            def pret(tb):
                ta, tt_ = tabr.next()
                c.dma(c.sp, ta[:, 0], dv[:, 0, :, tb * 512:(tb + 1) * 512], writes=[tt_])
                c.dma(c.sp, ta[:, 1], dv[:, 1, :, tb * 512:(tb + 1) * 512], writes=[tt_])
                return ta, tt_

            nxtt = pret(0)
            for tb in range(4):
                ta, tt_ = nxtt
                if tb + 1 < 4:
                    nxtt = pret(tb + 1)
                for g in range(8):
                    pa, pt = ps2.next()
                    for tt in range(16):
                        self.mm(pa, A[:, g, tt, 0:128], ta[:, 0, tt, :], tt == 0, False, [At[g], tt_], pt, sig=False)
                        self.mm(pa, A[:, g, tt, 128:256], ta[:, 1, tt, :], False, tt == 15, [At[g], tt_], pt)
                    sa, st = stg.next()
                    c.op(c.act, lambda: nc.scalar.mul(out=sa, in_=pa, mul=1.0 / 512.0), reads=[pt], writes=[st])
                    c.dma(c.sp, self.YF[g][:, tb * 512:(tb + 1) * 512], sa, reads=[st])
            sc_c = 1.0 / np.sqrt(TC * 128.0)
            for g in (range(8) if do_ctx else ()):
                pa, pt = ps2.next()
                for tt in range(2):
                    self.mm(pa[:, 0:256], A[:, g, 16 + tt, 0:128], tabc[:, 0, tt, :], tt == 0, False, [At[g], Ttc], pt, sig=False)
                    self.mm(pa[:, 0:256], A[:, g, 16 + tt, 128:256], tabc[:, 1, tt, :], False, tt == 1, [At[g], Ttc], pt)
                sa, st = stg.next()
                c.op(c.act, lambda: nc.scalar.mul(out=sa[:, 0:256], in_=pa[:, 0:256], mul=float(sc_c)), reads=[pt], writes=[st])
                c.dma(c.sp, self.YF[g][:, TL:NT], sa[:, 0:256], reads=[st])
            c.barrier()

    def phase_merge(self, l, do_ctx=True):
        nc, c = self.nc, self.c
        with ExitStack() as es:
            sbt = lambda name, shape, dt: es.enter_context(nc.sbuf_tensor(self.uniq(name), list(shape), dt)).ap()
            ys = [sbt(f"my{k}", [128, 8, NT], BF16) for k in range(3)]
            Ty = [T(f"my{k}") for k in range(3)]
            for k, src in enumerate((self.YA, self.YR, self.YF)):
                c.dma(c.sp, ys[k], src.rearrange("f p t -> p f t"), writes=[Ty[k]])
            wr = Ring([sbt(f"mw{i}", [128, 3, 8, 128], BF16) for i in range(3)], "mw")
            gr = Ring([sbt(f"mg{i}", [128, 3, NT], BF16) for i in range(2)], "mg")
            stg = Ring([sbt(f"mstg{i}", [128, NT], BF16) for i in range(2)], "mstg")
            tr = [Ring([sbt(f"mt{k}_{i}", [128, 512], F32) for i in range(2)], f"mt{k}") for k in range(3)]
            psr = self.psring([0, 1, 2, 3, 4, 5, 6, 7])
            wbv = self.w_branch[l]
            Gv = self.G.rearrange("(k dt) p t -> dt p k t", k=3)
            def pre(dt_):
                wa, wt = wr.next()
                c.dma(c.pool, wa, wbv[dt_], writes=[wt])
                ga, gt = gr.next()
                c.dma(c.sp, ga, Gv[dt_], writes=[gt])
                return wa, wt, ga, gt

            nxt = pre(0)
            for dt_ in range(NJ):
                wa, wt, ga, gt = nxt
                if dt_ + 1 < NJ:
                    nxt = pre(dt_ + 1)
                sa, st = stg.next()
                for (t0, n, v) in (TBS if do_ctx else TBS[:4]):
                    tk = []
                    for k in range(3):
                        pa, pt = psr.next()
                        for kc in range(8):
                            self.mm(pa[:, :n], wa[:, k, kc, :], ys[k][:, kc, t0:t0 + n], kc == 0, kc == 7, [wt, Ty[k]], pt)
                        ta, tt_ = tr[k].next()
                        c.op(c.dve, lambda: nc.vector.tensor_tensor(out=ta[:, :n], in0=pa[:, :n], in1=ga[:, k, t0:t0 + n], op=ALU.mult),
                             reads=[pt, gt], writes=[tt_])
                        tk.append((ta, tt_))
                    c.op(c.pool, lambda: nc.gpsimd.tensor_tensor(out=tk[0][0][:, :n], in0=tk[0][0][:, :n], in1=tk[1][0][:, :n], op=ALU.add),
                         reads=[tk[0][1], tk[1][1]], writes=[tk[0][1]])
                    c.op(c.pool, lambda: nc.gpsimd.tensor_tensor(out=sa[:, t0:t0 + n], in0=tk[0][0][:, :n], in1=tk[2][0][:, :n], op=ALU.add),
                         reads=[tk[0][1], tk[2][1]], writes=[st])
                c.dma(c.sp, self.MRG[dt_], sa, reads=[st])
            c.barrier()

    def proj_residual(self, src, nk, wview, gate_off, halves, side_mod=None):
        nc, c = self.nc, self.c
        for hi, blocks in enumerate(halves):
            h0 = blocks[0][0]
            hn = blocks[-1][0] + blocks[-1][1] - h0
            with ExitStack() as es:
                sbt = lambda name, shape, dt: es.enter_context(nc.sbuf_tensor(self.uniq(name), list(shape), dt)).ap()
                a = sbt("pa_act", [128, nk, hn], BF16)
                srcv = src.rearrange("f p t -> p f t")
                Tab = {}
                for (t0, n, v) in blocks:
                    Tab[t0] = T(f"pa_act{t0}")
                    c.dma(c.sp, a[:, :, t0 - h0:t0 - h0 + n], srcv[:, :, t0:t0 + n], writes=[Tab[t0]])
                wr = Ring([sbt(f"pw{i}", [128, nk, 128], BF16) for i in range(3)], "pw")
                xr = Ring([sbt(f"px{i}", [128, hn], F32) for i in range(3)], "px")
                psr = self.psring([0, 1, 2, 3, 4, 5, 6])
                if side_mod is not None:
                    emit_load, emit_mm = self.mod_emitter(side_mod, es)

                def pre(dt_):
                    wa, wt = wr.next()
                    c.dma(c.pool, wa, wview[dt_], writes=[wt])
                    xa, xt = xr.next()
                    c.dma(c.sp, xa, self.XT[dt_][:, h0:h0 + hn], writes=[xt])
                    return wa, wt, xa, xt

                nxt = pre(0)
                for dt_ in range(NJ):
                    wa, wt, xa, xt = nxt
                    if dt_ + 1 < NJ:
                        nxt = pre(dt_ + 1)
                    if side_mod is not None:
                        it = hi * NJ + dt_
                        g_lo, g_hi = (it * 3) // 2, ((it + 1) * 3) // 2
                        if dt_ == 0:
                            pend = []
                        for g in pend:
                            emit_mm(g)
                        for g in range(g_lo, g_hi):
                            emit_load(g)
                        pend = list(range(g_lo, g_hi))
                        if dt_ == NJ - 1:
                            for g in pend:
                                emit_mm(g)
                            pend = []
                    for (t0, n, v) in blocks:
                        pa, pt = psr.next()
                        for kc in range(nk):
                            self.mm(pa[:, :n], wa[:, kc, :], a[:, kc, t0 - h0:t0 - h0 + n], kc == 0, kc == nk - 1, [wt, Tab[t0]], pt)
                        c.op(c.dve, lambda: nc.vector.scalar_tensor_tensor(out=xa[:, t0 - h0:t0 - h0 + n], in0=pa[:, :n],
                                                                           scalar=self.mod[:, gate_off + dt_, v:v + 1],
                                                                           in1=xa[:, t0 - h0:t0 - h0 + n], op0=ALU.mult, op1=ALU.add),
                             reads=[pt, xt, self.Tmod], writes=[xt])
                    c.dma(c.sp, self.XT[dt_][:, h0:h0 + hn], xa, reads=[xt])
                c.barrier()

    def phase_ffn_in(self, l, side_mod=None, do_ctx=True):
        nc, c = self.nc, self.c
        with ExitStack() as es:
            sbt = lambda name, shape, dt: es.enter_context(nc.sbuf_tensor(self.uniq(name), list(shape), dt)).ap()
            h = sbt("h2", [128, NJ, NT], BF16)
            hts = [[T(f"h2{i}_{j}") for j in range(NJ)] for i in range(5)]
            wr = Ring([sbt(f"fw{i}", [128, NJ, 512], BF16) for i in range(3)], "fw")
            stg = Ring([sbt(f"fs{i}", [128, NT], BF16) for i in range(2)], "fs")
            sg = Ring([sbt(f"fsg{i}", [128, 512], F32) for i in range(2)], "fsg")
            GW = 128
            NG = 12288 // GW
            if side_mod is not None:
                emit_load, emit_mm = self.mod_emitter(side_mod, es, gw=GW)
            with ExitStack() as es2:
                self.norm_to_h(es2, self.A2, 48, h, hts, bw=256, nbuf=2, do_ctx=do_ctx)
            psg = self.psring([0, 1, 2, 3])
            psu = self.psring([4, 5, 6])
            wv = self.w_ffn_in[l].rearrange("(kc p) f -> p kc f", p=128)
            pend = []
            for grp in range(11):
                wga, wgt = wr.next()
                c.dma(c.pool, wga, wv[:, :, grp * 512:(grp + 1) * 512], writes=[wgt])
                wua, wut = wr.next()
                c.dma(c.pool, wua, wv[:, :, DFF + grp * 512:DFF + (grp + 1) * 512], writes=[wut])
                for ft in range(4):
                    f = grp * 4 + ft
                    sa, st = stg.next()
                    for bi, (t0, n, v) in enumerate(TBS if do_ctx else TBS[:4]):
                        if side_mod is not None:
                            tick = f * 5 + bi
                            if tick % 2 == 0 and tick // 2 < NG:
                                for g in pend:
                                    emit_mm(g)
                                emit_load(tick // 2)
                                pend = [tick // 2]
                        pg, pgt = psg.next()
                        for kc in range(NJ):
                            self.mm(pg[:, :n], wga[:, kc, ft * 128:(ft + 1) * 128], h[:, kc, t0:t0 + n], kc == 0, kc == NJ - 1, [hts[bi][kc], wgt], pgt)
                        pu, put = psu.next()
                        for kc in range(NJ):
                            self.mm(pu[:, :n], wua[:, kc, ft * 128:(ft + 1) * 128], h[:, kc, t0:t0 + n], kc == 0, kc == NJ - 1, [hts[bi][kc], wut], put)
                        ga, gt = sg.next()
                        c.op(c.act, lambda: nc.scalar.activation(out=ga[:, :n], in_=pg[:, :n], func=AF.Silu), reads=[pgt], writes=[gt])
                        c.op(c.dve, lambda: nc.vector.tensor_tensor(out=sa[:, t0:t0 + n], in0=pu[:, :n], in1=ga[:, :n], op=ALU.mult),
                             reads=[put, gt], writes=[st])
                    c.dma(c.sp, self.HID[f], sa, reads=[st])
            for g in pend:
                emit_mm(g)
            c.barrier()

    def phase_final(self):
        nc, c = self.nc, self.c
        with ExitStack() as es:
            sbt = lambda name, shape, dt: es.enter_context(nc.sbuf_tensor(self.uniq(name), list(shape), dt)).ap()
            xb = Ring([sbt(f"zxb{i}", [128, NJ, 512], F32) for i in range(2)], "zxb")
            sq = Ring([sbt("zsq0", [128, NJ, 512], F32)], "zsq")
            rs = Ring([sbt(f"zrs{i}", [128, 512], F32) for i in range(2)], "zrs")
            orow = Ring([sbt(f"zo{i}", [128, D], F32) for i in range(2)], "zo")
            psr = self.psring([6, 7])
            pst = self.psring([0, 1, 2, 3])
            XTv = self.XT.rearrange("j p t -> p j t")
            cb = self.colsb
            k = 0
            for (t0, n, v) in TBS[:4]:
                xa, xt = xb.next()
                c.dma(c.sp, xa, XTv[:, :, t0:t0 + n], writes=[xt])
                sa, st = sq.next()
                c.op(c.act, lambda: nc.scalar.activation(out=sa, in_=xa, func=AF.Square), reads=[xt], writes=[st])
                pa, pt = psr.next()
                for j in range(NJ):
                    self.mm(pa, self.ones32, sa[:, j, :], j == 0, j == NJ - 1, [st, self.Tconst], pt)
                ra, rt = rs.next()
                self.rstd_from_ps(pa, pt, ra, rt, 1.0 / D)
                for j in range(NJ):
                    c.op(c.dve, lambda: nc.vector.scalar_tensor_tensor(out=xa[:, j, :], in0=xa[:, j, :], scalar=cb[:, C_GF + j:C_GF + j + 1], in1=ra,
                                                                       op0=ALU.mult, op1=ALU.mult), reads=[xt, rt, self.Tcols], writes=[xt])
                for tt in range(4):
                    oa, ot = orow.next()
                    for j4 in range(4):
                        qa, qt = pst.next()
                        for jj in range(4):
                            j = j4 * 4 + jj
                            c.op(c.pe, lambda: nc.tensor.transpose(qa[:, jj * 128:(jj + 1) * 128], xa[:, j, tt * 128:(tt + 1) * 128], self.ident32),
                                 reads=[xt, self.Tconst], writes=[qt], sig=(jj == 3))
                        if k % 2 == 0:
                            c.op(c.act, lambda: nc.scalar.copy(out=oa[:, j4 * 512:(j4 + 1) * 512], in_=qa), reads=[qt], writes=[ot])
                        else:
                            c.op(c.dve, lambda: nc.vector.tensor_copy(out=oa[:, j4 * 512:(j4 + 1) * 512], in_=qa), reads=[qt], writes=[ot])
                        k += 1
                    r0 = t0 + tt * 128
                    c.dma(c.sp, self.out[r0:r0 + 128, :], oa, reads=[ot])
            c.barrier()

    def build(self, upto=None):
        self.phase_init()
        for l in range(self.n_layers):
            last = (l == self.n_layers - 1)
            dc = not last
            halves = HALF_BLOCKS if dc else HALF_BLOCKS_LAST
            self.phase_mod(l, standalone=(l == 0))
            self.phase_win(l)
            self.phase_attn(l, do_ctx=dc)
            self.phase_lru(l)
            self.phase_fourier(l, do_ctx=dc)
            self.phase_merge(l, do_ctx=dc)
            self.proj_residual(self.MRG, NJ, self.w_out[l], 32, halves)
            self.phase_ffn_in(l, side_mod=(l + 1 if not last else None), do_ctx=dc)
            self.proj_residual(self.HID, 44, self.w_ffn_out[l], 80, halves, side_mod=None)
        self.phase_final()
        return self.nc


def _col(v):
    v = np.asarray(v, np.float32)
    return np.ascontiguousarray(v.reshape(-1, 128).T)


def make_consts():
    bf = ml_dtypes.bfloat16
    t = np.arange(TL)
    row = (t // 64).astype(np.float64)
    colp = (t % 64).astype(np.float64)
    inv = 10000.0 ** (-np.arange(32, dtype=np.float64) / 32)
    ang = np.concatenate([row[:, None] * inv, colp[:, None] * inv], axis=-1)
    angT = np.concatenate([ang.T, ang.T], axis=0)
    k_rope = np.stack([np.cos(angT), np.sin(angT)]).astype(np.float32).astype(bf)
    ident = np.eye(128, dtype=np.float32)
    rotT = np.zeros((128, 128), np.float32)
    for m in range(64):
        rotT[m + 64, m] = -1.0
        rotT[m, m + 64] = 1.0
    cc = np.arange(128)
    ph = 2 * np.pi * ((cc[:, None] * cc[None, :]) % 128) / 128.0
    k_small = np.concatenate([ident, rotT, np.cos(ph), np.sin(ph)], axis=1).astype(np.float32).astype(bf)
    th = 2 * np.pi * ((t[:, None].astype(np.int64) * t[None, :]) % TL) / float(TL)
    k_dft = np.stack([np.cos(th), -np.sin(th)]).astype(np.float32).astype(bf)
    tc = np.arange(TC)
    thc = 2 * np.pi * ((tc[:, None] * tc[None, :]) % TC) / float(TC)
    k_dftc = np.stack([np.cos(thc), -np.sin(thc)]).astype(np.float32).astype(bf)
    return dict(k_rope=k_rope, k_small=k_small, k_ident32=ident, k_dft=k_dft, k_dftc=k_dftc)


def make_cols(inp):
    cols = np.zeros((DEPTH, 128, NCOL), np.float32)
    for l in range(DEPTH):
        cols[l, :, C_BMOD:C_BMOD + 96] = _col(inp["b_mod"][l])
        cols[l, :, C_GN1:C_GN1 + 16] = _col(inp["g_norm1"][l])
        cols[l, :, C_GN2:C_GN2 + 16] = _col(inp["g_norm2"][l])
        cols[l, :, C_BG:C_BG + 48] = _col(inp["b_gate"][l])
        for tap in range(4):
            cols[l, :, C_CW + tap * 8:C_CW + tap * 8 + 8] = _col(inp["conv_w"][l, tap])
        cols[l, :, C_CB:C_CB + 8] = _col(inp["conv_b"][l])
        for d in range(2):
            cols[l, :, C_BR + d * 8:C_BR + d * 8 + 8] = _col(inp["lru_b_r"][l, d])
            cols[l, :, C_BI + d * 8:C_BI + d * 8 + 8] = _col(inp["lru_b_i"][l, d])
            cols[l, :, C_LAM + d * 8:C_LAM + d * 8 + 8] = _col(inp["lru_lambda"][l, d])
        cols[l, :, C_QG] = inp["q_gain"][l]
        cols[l, :, C_KG] = inp["k_gain"][l]
        cols[l, :, C_GF:C_GF + 16] = _col(inp["g_final"])
    return cols


def make_in_maps(inp, cores):
    inp = {k: np.asarray(v) for k, v in inp.items()}
    consts = make_consts()
    cols = make_cols(inp)
    shared = dict(cols=cols, **consts)
    for k in ("w_mod", "w_in", "lru_w_r", "lru_w_i", "w_gate", "w_ffn_in"):
        shared[k] = np.ascontiguousarray(inp[k], dtype=np.float32)
    shared["w_out"] = np.ascontiguousarray(
        np.asarray(inp["w_out"], np.float32).reshape(DEPTH, NJ, 128, NJ, 128).transpose(0, 3, 2, 1, 4))
    shared["w_ffn_out"] = np.ascontiguousarray(
        np.asarray(inp["w_ffn_out"], np.float32).reshape(DEPTH, 44, 128, NJ, 128).transpose(0, 3, 2, 1, 4))
    shared["w_branch"] = np.ascontiguousarray(
        np.asarray(inp["w_branch"], np.float32).reshape(DEPTH, 3, 8, 128, NJ, 128).transpose(0, 4, 3, 1, 2, 5))
    maps = []
    for b in cores:
        cvec = np.stack([_col(inp["c"][b]), _col(inp["c_ctx"])], axis=-1)
        m = dict(shared)
        m["x"] = np.ascontiguousarray(inp["x"][b], dtype=np.float32)
        m["ctx"] = np.ascontiguousarray(inp["ctx"][b], dtype=np.float32)
        m["cvec"] = np.ascontiguousarray(cvec, dtype=np.float32)
        maps.append(m)
    return maps


def kernel(**inputs):
    n = 8
    nc = Builder(DEPTH).build()
    in_maps = make_in_maps(inputs, list(range(n)))
    res = run_bass_kernel_spmd(nc, in_maps, core_ids=list(range(n)))
    return np.stack([np.asarray(r["out"], dtype=np.float32) for r in res.results], axis=0)
```

```python
import numpy as np
import ml_dtypes
import concourse.bass as bass
import concourse.mybir as mybir
from contextlib import ExitStack
from concourse.bass_utils import run_bass_kernel_spmd

F32 = mybir.dt.float32
BF16 = mybir.dt.bfloat16
AF = mybir.ActivationFunctionType
ALU = mybir.AluOpType

D = 2048
TL = 2048
TC = 256
NT = TL + TC
DEPTH = 4
DIN = 4608
DFF = 5632
NJ = 16
EPS = 1e-6
SEM_ROT = 30000

TBS = [(0, 512, 0), (512, 512, 0), (1024, 512, 0), (1536, 512, 0), (2048, 256, 1)]
HALF_BLOCKS_LAST = [[(0, 512, 0), (512, 512, 0)], [(1024, 512, 0), (1536, 512, 0)]]
HALF_BLOCKS = [[(0, 512, 0), (512, 512, 0), (1024, 128, 0)], [(1152, 512, 0), (1664, 384, 0), (2048, 256, 1)]]

C_BMOD = 0
C_GN1 = 96
C_GN2 = 112
C_BG = 128
C_CW = 176
C_CB = 208
C_BR = 216
C_BI = 232
C_LAM = 248
C_QG = 264
C_KG = 265
C_GF = 266
NCOL = 282


class Tok:
    __slots__ = ("sem", "val", "eng")

    def __init__(self, sem=None, val=None, eng=None):
        self.sem = sem
        self.val = val
        self.eng = eng


class T:
    __slots__ = ("name", "w", "r")

    def __init__(self, name):
        self.name = name
        self.w = None
        self.r = []


class Eng:
    def __init__(self, ctx, name, h):
        self.ctx = ctx
        self.name = name
        self.h = h
        self.sem = None
        self.cnt = 0
        self.nsem = 0
        self.waited = {}
        self.pending = []
        self.last = None
        self.n_ins = 0
        self.n_wait = 0

    def cur_sem(self):
        if self.sem is None or self.cnt >= SEM_ROT:
            self.sem = self.ctx.nc.alloc_semaphore(f"s_{self.name}_{self.nsem}")
            self.nsem += 1
            self.cnt = 0
        return self.sem


class Ctx:
    def __init__(self, nc, n_dma_sems=10):
        self.nc = nc
        self.pe = Eng(self, "pe", nc.tensor)
        self.act = Eng(self, "act", nc.scalar)
        self.dve = Eng(self, "dve", nc.vector)
        self.pool = Eng(self, "pool", nc.gpsimd)
        self.sp = Eng(self, "sp", nc.sync)
        self.engs = [self.pe, self.act, self.dve, self.pool, self.sp]
        self.dsems = {}
        for e in (self.sp, self.pool):
            self.dsems[e.name] = [[nc.alloc_semaphore(f"d_{e.name}_{i}"), 0, None] for i in range(n_dma_sems)]
        self.drr = {e.name: 0 for e in (self.sp, self.pool)}
        self.n_dma = 0

    def _wait(self, eng, tok):
        if tok is None:
            return
        if tok.eng is eng and eng is self.pe:
            return
        if tok.sem is None:
            raise RuntimeError(f"unresolved dependency needed by {eng.name}")
        key = tok.sem.name
        if eng.waited.get(key, 0) >= tok.val:
            return
        eng.h.wait_ge(tok.sem, tok.val)
        eng.waited[key] = tok.val
        eng.n_wait += 1

    def _deps(self, eng, reads, writes):
        for t in reads:
            self._wait(eng, t.w)
        for t in writes:
            self._wait(eng, t.w)
            for r in t.r:
                self._wait(eng, r)

    def _mark(self, tok, reads, writes):
        for t in reads:
            t.r = [r for r in t.r if not (r.sem is not None and tok.sem is not None and r.sem is tok.sem and r.val <= tok.val)]
            t.r.append(tok)
        for t in writes:
            t.w = tok
            t.r = []

    def op(self, eng, fn, reads=(), writes=(), sig=True):
        self._deps(eng, reads, writes)
        ins = fn()
        eng.n_ins += 1
        tok = Tok(eng=eng)
        if sig:
            sem = eng.cur_sem()
            ins.then_inc(sem, 1)
            eng.cnt += 1
            tok.sem = sem
            tok.val = eng.cnt
            for p in eng.pending:
                p.sem = sem
                p.val = eng.cnt
            eng.pending = []
            eng.last = tok
        else:
            eng.pending.append(tok)
        self._mark(tok, reads, writes)
        return tok

    def dma(self, q, out, in_, reads=(), writes=()):
        self._deps(q, reads, writes)
        lst = self.dsems[q.name]
        i = self.drr[q.name]
        self.drr[q.name] = (i + 1) % len(lst)
        slot = lst[i]
        if slot[2] is not None:
            self._wait(q, slot[2])
        ins = q.h.dma_start(out=out, in_=in_)
        slot[1] += 16
        ins.then_inc(slot[0], 16)
        tok = Tok(sem=slot[0], val=slot[1], eng=None)
        slot[2] = tok
        self.n_dma += 1
        self._mark(tok, reads, writes)
        return tok

    def barrier(self):
        toks = []
        for e in self.engs:
            assert not e.pending, f"pending nosig ops on {e.name} at barrier"
            if e.last is not None:
                toks.append(e.last)
        for lst in self.dsems.values():
            for slot in lst:
                if slot[2] is not None:
                    toks.append(slot[2])
        for e in self.engs:
            for t in toks:
                if t.eng is e and e is self.pe:
                    continue
                self._wait(e, t)

    def stats(self):
        d = {e.name: (e.n_ins, e.n_wait, e.nsem) for e in self.engs}
        d["dma"] = self.n_dma
        return d


class Ring:
    def __init__(self, aps, name):
        self.aps = aps
        self.ts = [T(f"{name}{i}") for i in range(len(aps))]
        self.i = 0

    def next(self):
        k = self.i % len(self.aps)
        self.i += 1
        return self.aps[k], self.ts[k]


class Builder:
    def __init__(self, n_layers=DEPTH, dbg=()):
        self.n_layers = n_layers
        self.dbg = set(dbg)
        nc = self.nc = bass.Bass("TRN2", target_bir_lowering=False)
        self.c = Ctx(nc)
        inp = lambda name, shape, dt=F32: nc.dram_tensor(name, list(shape), dt, kind="ExternalInput").ap()
        self.x = inp("x", [TL, D])
        self.ctx_in = inp("ctx", [TC, D])
        self.cvec = inp("cvec", [128, NJ, 2])
        self.cols = inp("cols", [DEPTH, 128, NCOL])
        self.w_mod = inp("w_mod", [DEPTH, D, 6 * D])
        self.w_in = inp("w_in", [DEPTH, D, DIN])
        self.lru_w_r = inp("lru_w_r", [DEPTH, 2, 16, 64, 64])
        self.lru_w_i = inp("lru_w_i", [DEPTH, 2, 16, 64, 64])
        self.w_branch = inp("w_branch", [DEPTH, NJ, 128, 3, 8, 128])
        self.w_gate = inp("w_gate", [DEPTH, D, 3 * D])
        self.w_out = inp("w_out", [DEPTH, NJ, 128, NJ, 128])
        self.w_ffn_in = inp("w_ffn_in", [DEPTH, D, 2 * DFF])
        self.w_ffn_out = inp("w_ffn_out", [DEPTH, NJ, 128, 44, 128])
        self.k_rope = inp("k_rope", [2, 128, TL], BF16)
        self.k_small = inp("k_small", [128, 128 * 2 + 256], BF16)
        self.k_ident32 = inp("k_ident32", [128, 128])
        self.k_dft = inp("k_dft", [2, TL, TL], BF16)
        self.k_dftc = inp("k_dftc", [2, TC, TC], BF16)
        self.out = nc.dram_tensor("out", [TL, D], F32, kind="ExternalOutput").ap()
        self.XT = self.scr("XT", [NJ, 128, NT], F32)
        self.QK = self.scr("QK", [10, 128, NT], BF16)
        self.VT = self.scr("VT", [18, 128, 256], BF16)
        self.U = self.scr("U", [24, 128, NT], BF16)
        self.G = self.scr("G", [48, 128, NT], BF16)
        self.YA = self.scr("YA", [8, 128, NT], BF16)
        self.YR = self.scr("YR", [8, 128, NT], BF16)
        self.YF = self.scr("YF", [8, 128, NT], BF16)
        self.MRG = self.scr("MRG", [NJ, 128, NT], BF16)
        self.HID = self.scr("HID", [44, 128, NT], BF16)
        self.ps = [nc.alloc_psum_tensor(f"ps{i}", [128, 512], F32).ap() for i in range(8)]
        self.pst = [T(f"ps{i}") for i in range(8)]
        sb = lambda name, shape, dt: nc.alloc_sbuf_tensor(name, list(shape), dt).ap()
        self.ident32 = sb("ident32", [128, 128], F32)
        self.ones32 = sb("ones32", [128, 128], F32)
        self.ksm = sb("ksm", [128, 512], BF16)
        self.ones_bf = sb("ones_bf", [128, 128], BF16)
        self.epsc = sb("epsc", [128, 1], F32)
        self.colsb = sb("colsb", [128, NCOL], F32)
        self.mod = sb("mod", [128, 96, 2], F32)
        self.A1 = sb("A1", [128, NJ, 2], F32)
        self.A2 = sb("A2", [128, NJ, 2], F32)
        self.nsp = sb("nsp", [128, 16], F32)
        self.nsp2 = sb("nsp2", [128, 16], F32)
        self.cs = sb("cs", [128, NJ, 2], BF16)
        self.cv32 = sb("cv32", [128, NJ, 2], F32)
        self.Tconst = T("const")
        self.Tcols = T("cols")
        self.Tmod = T("mod")

    def uniq(self, name):
        self._uid = getattr(self, "_uid", 0) + 1
        return f"{name}_{self._uid}"

    def scr(self, name, shape, dt):
        kind = "ExternalOutput" if name in self.dbg else "Internal"
        return self.nc.dram_tensor("scr_" + name, list(shape), dt, kind=kind).ap()

    def psring(self, idxs, name="psr"):
        r = Ring([self.ps[i] for i in idxs], name)
        r.ts = [self.pst[i] for i in idxs]
        return r

    def mm(self, out, lhsT, rhs, start, stop, reads, wt, sig=None):
        nc = self.nc
        if sig is None:
            sig = stop
        return self.c.op(self.c.pe, lambda: nc.tensor.matmul(out, lhsT=lhsT, rhs=rhs, start=start, stop=stop),
                         reads=reads, writes=[wt], sig=sig)

    def rstd_from_ps(self, ps_ap, ps_t, out_ap, out_t, scale):
        nc, c = self.nc, self.c
        c.op(c.act, lambda: nc.scalar.activation(out=out_ap, in_=ps_ap, func=AF.Sqrt, bias=self.epsc[:, 0:1], scale=scale),
             reads=[ps_t, self.Tconst], writes=[out_t])
        c.op(c.dve, lambda: nc.vector.reciprocal(out=out_ap, in_=out_ap), reads=[out_t], writes=[out_t])

    def phase_init(self):
        nc, c = self.nc, self.c
        with ExitStack() as es:
            sbt = lambda name, shape, dt: es.enter_context(nc.sbuf_tensor(self.uniq(name), list(shape), dt)).ap()
            c.dma(c.sp, self.ident32, self.k_ident32, writes=[self.Tconst])
            c.dma(c.sp, self.ksm, self.k_small, writes=[self.Tconst])
            c.op(c.dve, lambda: nc.vector.memset(self.ones32, 1.0), writes=[self.Tconst])
            c.op(c.dve, lambda: nc.vector.memset(self.ones_bf, 1.0), writes=[self.Tconst])
            c.op(c.dve, lambda: nc.vector.memset(self.epsc, EPS), writes=[self.Tconst])
            c.dma(c.sp, self.cv32, self.cvec, writes=[self.Tconst])
            c.op(c.act, lambda: nc.scalar.activation(out=self.cs, in_=self.cv32, func=AF.Silu), reads=[self.Tconst], writes=[self.Tconst])
            xin = Ring([sbt(f"xin{i}", [128, D], F32) for i in range(2)], "xin")
            xo = Ring([sbt(f"xo{i}", [128, NJ, 128], F32) for i in range(2)], "xo")
            psr = self.psring([0, 1, 2, 3])
            XTv = self.XT.rearrange("j p t -> p j t")
            k = 0
            for tt in range(18):
                src = self.x[tt * 128:(tt + 1) * 128, :] if tt < 16 else self.ctx_in[(tt - 16) * 128:(tt - 15) * 128, :]
                xa, xt = xin.next()
                c.dma(c.sp, xa, src, writes=[xt])
                oa, ot = xo.next()
                for j4 in range(4):
                    pa, pt = psr.next()
                    for jj in range(4):
                        j = j4 * 4 + jj
                        c.op(c.pe, lambda: nc.tensor.transpose(pa[:, jj * 128:(jj + 1) * 128], xa[:, j * 128:(j + 1) * 128], self.ident32),
                             reads=[xt, self.Tconst], writes=[pt], sig=(jj == 3))
                    dst = oa[:, j4 * 4:(j4 + 1) * 4, :]
                    src_ps = pa.rearrange("p (a b) -> p a b", b=128)
                    if k % 2 == 0:
                        c.op(c.act, lambda: nc.scalar.copy(out=dst, in_=src_ps), reads=[pt], writes=[ot])
                    else:
                        c.op(c.dve, lambda: nc.vector.tensor_copy(out=dst, in_=src_ps), reads=[pt], writes=[ot])
                    k += 1
                c.dma(c.sp, XTv[:, :, tt * 128:(tt + 1) * 128], oa, reads=[ot])
            c.barrier()

    def mod_emitter(self, l, es, gw=256, cast_on_pool=False):
        nc, c = self.nc, self.c
        sbt = lambda name, shape, dt: es.enter_context(nc.sbuf_tensor(self.uniq(name), list(shape), dt)).ap()
        f32r = Ring([sbt(f"wmf{i}", [128, NJ, gw], F32) for i in range(2)], "wmf")
        bfr = Ring([sbt(f"wmb{i}", [128, NJ, gw], BF16) for i in range(2)], "wmb")
        wv = self.w_mod[l].rearrange("(kc p) f -> p kc f", p=128)
        pm, pmt = self.ps[7], self.pst[7]
        stash = {}

        def emit_load(g):
            fa, ft_ = f32r.next()
            c.dma(c.sp, fa, wv[:, :, g * gw:(g + 1) * gw], writes=[ft_])
            ba, bt = bfr.next()
            if cast_on_pool:
                c.op(c.pool, lambda: nc.gpsimd.tensor_copy(out=ba, in_=fa), reads=[ft_], writes=[bt])
            else:
                c.op(c.act, lambda: nc.scalar.copy(out=ba, in_=fa), reads=[ft_], writes=[bt])
            stash[g] = (ba, bt)

        def emit_mm(g):
            ba, bt = stash.pop(g)
            nft = gw // 128
            for ft in range(nft):
                f = g * nft + ft
                for kc in range(NJ):
                    self.mm(pm[:, f * 2:(f + 1) * 2], ba[:, kc, ft * 128:(ft + 1) * 128], self.cs[:, kc, :],
                            kc == 0, kc == NJ - 1, [bt, self.Tconst], pmt, sig=(kc == NJ - 1 and ft == nft - 1))

        return emit_load, emit_mm

    def phase_mod(self, l, standalone):
        nc, c = self.nc, self.c
        with ExitStack() as es:
            c.dma(c.sp, self.colsb, self.cols[l], writes=[self.Tcols])
            pm, pmt = self.ps[7], self.pst[7]
            if standalone:
                emit_load, emit_mm = self.mod_emitter(l, es)
                for g in range(48):
                    emit_load(g)
                    if g > 0:
                        emit_mm(g - 1)
                emit_mm(47)
            pmv = pm[:, 0:192].rearrange("p (f v) -> p f v", v=2)
            cb = self.colsb
            for v in range(2):
                c.op(c.dve, lambda: nc.vector.tensor_tensor(out=self.mod[:, :, v], in0=pmv[:, :, v], in1=cb[:, C_BMOD:C_BMOD + 96], op=ALU.add),
                     reads=[pmt, self.Tcols], writes=[self.Tmod])
            for v in range(2):
                c.op(c.dve, lambda: nc.vector.scalar_tensor_tensor(out=self.A1[:, :, v], in0=self.mod[:, 16:32, v], scalar=1.0,
                                                                   in1=cb[:, C_GN1:C_GN1 + 16], op0=ALU.add, op1=ALU.mult),
                     reads=[self.Tmod, self.Tcols], writes=[self.Tmod])
                c.op(c.dve, lambda: nc.vector.scalar_tensor_tensor(out=self.A2[:, :, v], in0=self.mod[:, 64:80, v], scalar=1.0,
                                                                   in1=cb[:, C_GN2:C_GN2 + 16], op0=ALU.add, op1=ALU.mult),
                     reads=[self.Tmod, self.Tcols], writes=[self.Tmod])
            c.op(c.act, lambda: nc.scalar.activation(out=self.nsp, in_=cb[:, C_LAM:C_LAM + 16], func=AF.Exp, scale=-1.0),
                 reads=[self.Tcols], writes=[self.Tmod])
            c.op(c.act, lambda: nc.scalar.activation(out=self.nsp, in_=self.nsp, func=AF.Ln, bias=1.0, scale=1.0),
                 reads=[self.Tmod], writes=[self.Tmod])
            c.op(c.dve, lambda: nc.vector.tensor_scalar_mul(out=self.nsp2, in0=self.nsp, scalar1=-16.0), reads=[self.Tmod], writes=[self.Tmod])
            c.op(c.dve, lambda: nc.vector.tensor_scalar_mul(out=self.nsp, in0=self.nsp, scalar1=-8.0), reads=[self.Tmod], writes=[self.Tmod])
            c.barrier()

    def norm_to_h(self, es, A, Boff, h, hts, bw=512, nbuf=3, do_ctx=True):
        nc, c = self.nc, self.c
        sbt = lambda name, shape, dt: es.enter_context(nc.sbuf_tensor(self.uniq(name), list(shape), dt)).ap()
        xb_aps = [sbt(f"nxb{i}", [128, NJ, bw], F32) for i in range(nbuf)]
        xb_ts = [[T(f"nxb{i}_{j}") for j in range(NJ)] for i in range(nbuf)]
        sq = Ring([sbt("nsq0", [128, NJ, bw], BF16)], "nsq")
        rs = Ring([sbt(f"nrs{i}", [128, bw], F32) for i in range(2)], "nrs")
        psr = self.psring([5, 6])
        XTv = self.XT.rearrange("j p t -> p j t")
        stash = {}
        blocks = []
        for bi, (t0, n, v) in enumerate(TBS if do_ctx else TBS[:4]):
            for off in range(0, n, bw):
                blocks.append((t0 + off, min(bw, n - off), v, bi))

        def stage1(k):
            t0, n, v, bi = blocks[k]
            xa, xts = xb_aps[k % nbuf], xb_ts[k % nbuf]
            c.dma(c.sp, xa[:, :, :n], XTv[:, :, t0:t0 + n], writes=xts)
            sa, st = sq.next()
            c.op(c.act, lambda: nc.scalar.activation(out=sa[:, :, :n], in_=xa[:, :, :n], func=AF.Square), reads=xts, writes=[st])
            pa, pt = psr.next()
            for j in range(NJ):
                self.mm(pa[:, :n], self.ones_bf, sa[:, j, :n], j == 0, j == NJ - 1, [st, self.Tconst], pt)
            stash[k] = (xa, xts, pa, pt)

        def stage2(k):
            t0, n, v, bi = blocks[k]
            xa, xts, pa, pt = stash.pop(k)
            ra, rt = rs.next()
            self.rstd_from_ps(pa[:, :n], pt, ra[:, :n], rt, 1.0 / D)
            for j in range(NJ):
                c.op(c.dve, lambda: nc.vector.scalar_tensor_tensor(out=xa[:, j, :n], in0=xa[:, j, :n], scalar=A[:, j, v:v + 1], in1=ra[:, :n],
                                                                   op0=ALU.mult, op1=ALU.mult), reads=[xts[j], rt, self.Tmod], writes=[xts[j]])
                c.op(c.act, lambda: nc.scalar.activation(out=h[:, j, t0:t0 + n], in_=xa[:, j, :n], func=AF.Identity,
                                                         bias=self.mod[:, Boff + j, v:v + 1], scale=1.0),
                     reads=[xts[j], self.Tmod], writes=[hts[bi][j]])

        stage1(0)
        for k in range(len(blocks)):
            if k + 1 < len(blocks):
                stage1(k + 1)
            stage2(k)

    def phase_win(self, l):
        nc, c = self.nc, self.c
        with ExitStack() as es:
            sbt = lambda name, shape, dt: es.enter_context(nc.sbuf_tensor(self.uniq(name), list(shape), dt)).ap()
            h = sbt("h", [128, NJ, NT], BF16)
            hts = [[T(f"h{i}_{j}") for j in range(NJ)] for i in range(5)]
            with ExitStack() as es2:
                self.norm_to_h(es2, self.A1, 0, h, hts)
                c.barrier()
            wr = Ring([sbt(f"ww{i}", [128, NJ, 512], BF16) for i in range(3)], "ww")
            stg = Ring([sbt(f"stg{i}", [128, NT], BF16) for i in range(3)], "stg")
            qf = Ring([sbt(f"qf{i}", [128, 512], F32) for i in range(4)], "qf")
            sqq = Ring([sbt(f"sqq{i}", [128, 512], BF16) for i in range(4)], "sqq")
            rsq = Ring([sbt(f"rsq{i}", [128, 512], F32) for i in range(4)], "rsq")
            qn = Ring([sbt(f"qn{i}", [128, 512], BF16) for i in range(4)], "qn")
            t1r = Ring([sbt(f"t1r{i}", [128, 512], F32) for i in range(2)], "t1r")
            t2r = Ring([sbt(f"t2r{i}", [128, 512], F32) for i in range(2)], "t2r")
            vtr = Ring([sbt(f"vtr{i}", [128, 256], BF16) for i in range(3)], "vtr")
            rope = sbt("rope", [128, 2, TL], BF16)
            Trope = T("rope")
            c.dma(c.sp, rope, self.k_rope.rearrange("a p t -> p a t"), writes=[Trope])
            psm = self.psring([0, 1, 2, 3])
            ps2 = self.psring([4, 5])
            ps3 = self.psring([6, 7])
            rotT = self.ksm[:, 128:256]
            cb = self.colsb
            wv = self.w_in[l].rearrange("(kc p) f -> p kc f", p=128)
            wgs = []
            for grp in range(3):
                wa, wt = wr.next()
                c.dma(c.pool, wa, wv[:, :, grp * 512:(grp + 1) * 512], writes=[wt])
                wgs.append((wa, wt))
            items = [(f, bi) for f in range(10) for bi in range(5)]
            stA, stB, fstage, fdone = {}, {}, {}, {}

            def finish_block(f):
                fdone[f] = fdone.get(f, 0) + 1
                if fdone[f] == 5:
                    sa, st = fstage.pop(f)
                    c.dma(c.sp, self.QK[f], sa, reads=[st])

            def stageA(i):
                f, bi = items[i]
                t0, n, v = TBS[bi]
                wa, wt = wgs[f // 4]
                ft = f % 4
                if f not in fstage:
                    fstage[f] = stg.next()
                pa, pt = psm.next()
                for kc in range(NJ):
                    self.mm(pa[:, :n], wa[:, kc, ft * 128:(ft + 1) * 128], h[:, kc, t0:t0 + n], kc == 0, kc == NJ - 1, [hts[bi][kc], wt], pt)
                qa, qt = qf.next()
                c.op(c.act, lambda: nc.scalar.copy(out=qa[:, :n], in_=pa[:, :n]), reads=[pt], writes=[qt])
                s2, s2t = sqq.next()
                c.op(c.act, lambda: nc.scalar.activation(out=s2[:, :n], in_=pa[:, :n], func=AF.Square), reads=[pt], writes=[s2t])
                stA[i] = (qa, qt, s2, s2t)

            def stageB(i):
                f, bi = items[i]
                t0, n, v = TBS[bi]
                qa, qt, s2, s2t = stA.pop(i)
                sa, st = fstage[f]
                gcol = cb[:, C_QG:C_QG + 1] if f < 8 else cb[:, C_KG:C_KG + 1]
                p2, p2t = ps2.next()
                self.mm(p2[:, :n], self.ones_bf, s2[:, :n], True, True, [s2t, self.Tconst], p2t)
                ra, rt = rsq.next()
                self.rstd_from_ps(p2[:, :n], p2t, ra[:, :n], rt, 1.0 / 128)
                if v == 1:
                    c.op(c.dve, lambda: nc.vector.scalar_tensor_tensor(out=sa[:, t0:t0 + n], in0=qa[:, :n], scalar=gcol, in1=ra[:, :n],
                                                                       op0=ALU.mult, op1=ALU.mult), reads=[qt, rt, self.Tcols], writes=[st])
                    finish_block(f)
                else:
                    na, nt_ = qn.next()
                    c.op(c.dve, lambda: nc.vector.scalar_tensor_tensor(out=na[:, :n], in0=qa[:, :n], scalar=gcol, in1=ra[:, :n],
                                                                       op0=ALU.mult, op1=ALU.mult), reads=[qt, rt, self.Tcols], writes=[nt_])
                    stB[i] = (na, nt_)

            def stageC(i):
                f, bi = items[i]
                t0, n, v = TBS[bi]
                if v == 1:
                    return
                na, nt_ = stB.pop(i)
                sa, st = fstage[f]
                p3, p3t = ps3.next()
                self.mm(p3[:, :n], rotT, na[:, :n], True, True, [nt_, self.Tconst], p3t)
                a1, a1t = t1r.next()
                c.op(c.pool, lambda: nc.gpsimd.tensor_tensor(out=a1[:, :n], in0=na[:, :n], in1=rope[:, 0, t0:t0 + n], op=ALU.mult),
                     reads=[nt_, Trope], writes=[a1t])
                a2, a2t = t2r.next()
                c.op(c.dve, lambda: nc.vector.tensor_tensor(out=a2[:, :n], in0=p3[:, :n], in1=rope[:, 1, t0:t0 + n], op=ALU.mult),
                     reads=[p3t, Trope], writes=[a2t])
                c.op(c.pool, lambda: nc.gpsimd.tensor_tensor(out=sa[:, t0:t0 + n], in0=a1[:, :n], in1=a2[:, :n], op=ALU.add),
                     reads=[a1t, a2t], writes=[st])
                finish_block(f)

            NI = len(items)
            for i in range(NI + 2):
                if i < NI:
                    stageA(i)
                if 0 <= i - 1 < NI:
                    stageB(i - 1)
                if 0 <= i - 2 < NI:
                    stageC(i - 2)
            wa, wt = wgs[2]
            for tt in range(18):
                pa, pt = psm.next()
                for kc in range(NJ):
                    self.mm(pa[:, 0:256], h[:, kc, tt * 128:(tt + 1) * 128], wa[:, kc, 256:512], kc == 0, kc == NJ - 1,
                            [hts[min(tt // 4, 4)][kc], wt], pt)
                va, vt = vtr.next()
                c.op(c.act, lambda: nc.scalar.copy(out=va, in_=pa[:, 0:256]), reads=[pt], writes=[vt])
                c.dma(c.sp, self.VT[tt], va, reads=[vt])
            ke = 0
            for grp in range(3, 9):
                wa, wt = wr.next()
                c.dma(c.pool, wa, wv[:, :, grp * 512:(grp + 1) * 512], writes=[wt])
                for ft in range(4):
                    f = grp * 4 + ft
                    sa, st = stg.next()
                    for bi, (t0, n, v) in enumerate(TBS):
                        pa, pt = psm.next()
                        for kc in range(NJ):
                            self.mm(pa[:, :n], wa[:, kc, ft * 128:(ft + 1) * 128], h[:, kc, t0:t0 + n], kc == 0, kc == NJ - 1, [hts[bi][kc], wt], pt)
                        if ke % 2 == 0:
                            c.op(c.act, lambda: nc.scalar.copy(out=sa[:, t0:t0 + n], in_=pa[:, :n]), reads=[pt], writes=[st])
                        else:
                            c.op(c.dve, lambda: nc.vector.tensor_copy(out=sa[:, t0:t0 + n], in_=pa[:, :n]), reads=[pt], writes=[st])
                        ke += 1
                    c.dma(c.sp, self.U[f - 12], sa, reads=[st])
            wg = self.w_gate[l].rearrange("(kc p) f -> p kc f", p=128)
            for grp in range(12):
                wa, wt = wr.next()
                c.dma(c.pool, wa, wg[:, :, grp * 512:(grp + 1) * 512], writes=[wt])
                for ft in range(4):
                    f = grp * 4 + ft
                    sa, st = stg.next()
                    for bi, (t0, n, v) in enumerate(TBS):
                        pa, pt = psm.next()
                        for kc in range(NJ):
                            self.mm(pa[:, :n], wa[:, kc, ft * 128:(ft + 1) * 128], h[:, kc, t0:t0 + n], kc == 0, kc == NJ - 1, [hts[bi][kc], wt], pt)
                        c.op(c.act, lambda: nc.scalar.activation(out=sa[:, t0:t0 + n], in_=pa[:, :n], func=AF.Sigmoid,
                                                                 bias=cb[:, C_BG + f:C_BG + f + 1], scale=1.0), reads=[pt, self.Tcols], writes=[st])
                    c.dma(c.sp, self.G[f], sa, reads=[st])
            c.barrier()

    def phase_attn(self, l, do_ctx=True):
        nc, c = self.nc, self.c
        with ExitStack() as es:
            sbt = lambda name, shape, dt: es.enter_context(nc.sbuf_tensor(self.uniq(name), list(shape), dt)).ap()
            q = sbt("q", [128, 8, NT], BF16)
            k = sbt("k", [128, 2, NT], BF16)
            vt = sbt("vt", [128, 18, 256], BF16)
            Tq, Tk, Tv = T("q"), T("k"), T("v")
            QKv = self.QK.rearrange("f p t -> p f t")
            c.dma(c.sp, k, QKv[:, 8:10, :], writes=[Tk])
            c.dma(c.sp, vt, self.VT.rearrange("a p e -> p a e"), writes=[Tv])
            c.dma(c.sp, q, QKv[:, 0:8, :], writes=[Tq])
            ptr = Ring([sbt(f"pt{i}", [128, 512], BF16) for i in range(4)], "pt")
            rl = Ring([sbt(f"rl{i}", [128, 512], F32) for i in range(2)], "rl")
            stg = Ring([sbt(f"ostg{i}", [128, 512], BF16) for i in range(3)], "ostg")
            pss = self.psring([0, 1, 2, 3])
            pso = self.psring([4, 5])
            psl = self.psring([6, 7])
            scale = 1.0 / np.sqrt(128.0)

            def attend(hd, q0, nq, kts):
                g = hd // 4
                oa, ot = pso.next()
                la, lt = psl.next()
                nk = len(kts)
                S = []

                def emit_s(i):
                    kt = kts[i]
                    sa, st = pss.next()
                    self.mm(sa[:, :nq], k[:, g, kt * 128:(kt + 1) * 128], q[:, hd, q0:q0 + nq], True, True, [Tk, Tq], st)
                    S.append((sa, st))

                emit_s(0)
                if nk > 1:
                    emit_s(1)
                for i, kt in enumerate(kts):
                    sa, st = S[i]
                    pa, pt_ = ptr.next()
                    c.op(c.act, lambda: nc.scalar.activation(out=pa[:, :nq], in_=sa[:, :nq], func=AF.Exp, scale=scale), reads=[st], writes=[pt_])
                    if i + 2 < nk:
                        emit_s(i + 2)
                    self.mm(oa[:, :nq], vt[:, kt, g * 128:(g + 1) * 128], pa[:, :nq], i == 0, i == nk - 1, [Tv, pt_], ot)
                    self.mm(la[:, :nq], self.ones_bf, pa[:, :nq], i == 0, i == nk - 1, [self.Tconst, pt_], lt, sig=True)
                ra, rt = rl.next()
                c.op(c.dve, lambda: nc.vector.reciprocal(out=ra[:, :nq], in_=la[:, :nq]), reads=[lt], writes=[rt])
                ga, gt = stg.next()
                c.op(c.dve, lambda: nc.vector.tensor_tensor(out=ga[:, :nq], in0=oa[:, :nq], in1=ra[:, :nq], op=ALU.mult), reads=[ot, rt], writes=[gt])
                c.dma(c.sp, self.YA[hd][:, q0:q0 + nq], ga[:, :nq], reads=[gt])

            for hd in range(8):
                for qb in range(4):
                    attend(hd, qb * 512, 512, list(range(18)))
                if do_ctx:
                    attend(hd, TL, TC, [16, 17])
            c.barrier()

    def phase_lru(self, l):
        nc, c = self.nc, self.c
        with ExitStack() as es:
            sbt = lambda name, shape, dt: es.enter_context(nc.sbuf_tensor(self.uniq(name), list(shape), dt)).ap()
            cb = self.colsb
            bd = sbt("bd", [128, 2, 2, 8, 128], BF16)
            Tbd = T("bd")
            c.op(c.pool, lambda: nc.gpsimd.memset(bd, 0.0), writes=[Tbd])
            for d in range(2):
                for ri, w in enumerate((self.lru_w_r, self.lru_w_i)):
                    wv = w[l, d].rearrange("(j two) c e -> two c j e", two=2)
                    c.dma(c.pool, bd[0:64, d, ri, :, 0:64], wv[0], writes=[Tbd])
                    c.dma(c.pool, bd[64:128, d, ri, :, 64:128], wv[1], writes=[Tbd])
            mk = lambda name, dt, k=1, w=NT: Ring([sbt(f"{name}{i}", [128, w], dt) for i in range(k)], name)
            ur = mk("lu", BF16, 3)
            ugr = mk("lug", BF16, 3)
            vr = mk("lv", F32, 2)
            vbr = mk("lvb", BF16, 2)
            rr = mk("lr", F32, 2)
            ir = mk("li", F32, 2)
            ar = mk("la", F32, 2)
            er = mk("le", F32, 2)
            hr = mk("lh", F32, 2)
            yr = mk("ly", F32, 2)
            gr = mk("lg", F32, 1)
            sr = mk("lst", BF16, 2)
            psr = self.psring([0, 1, 2, 3, 4, 5, 6, 7])
            segs = [(0, TL), (TL, TC)]

            def rev(ap, t0, n):
                a = ap[:, t0:t0 + n]
                pstep = a.ap[0][0]
                return bass.AP(a.tensor, a.offset + (n - 1), [[pstep, 128], [-1, n]])

            def load(j):
                ua, ut = ur.next()
                c.dma(c.sp, ua, self.U[j], writes=[ut])
                uga, ugt = ugr.next()
                c.dma(c.sp, uga, self.U[8 + j], writes=[ugt])
                return ua, ut, uga, ugt

            def prep(j, ld):
                ua, ut, uga, ugt = ld
                va, vt_ = vr.next()
                wcol = lambda tap: cb[:, C_CW + tap * 8 + j:C_CW + tap * 8 + j + 1]
                c.op(c.dve, lambda: nc.vector.tensor_scalar(out=va, in0=ua, scalar1=wcol(2), scalar2=cb[:, C_CB + j:C_CB + j + 1],
                                                            op0=ALU.mult, op1=ALU.add), reads=[ut, self.Tcols], writes=[vt_])
                for (s0, sn) in segs:
                    for tap, sh in ((0, -2), (1, -1), (3, 1)):
                        lo = max(0, -sh)
                        hi = sn - max(0, sh)
                        c.op(c.dve, lambda: nc.vector.scalar_tensor_tensor(out=va[:, s0 + lo:s0 + hi], in0=ua[:, s0 + lo + sh:s0 + hi + sh], scalar=wcol(tap),
                                                                           in1=va[:, s0 + lo:s0 + hi], op0=ALU.mult, op1=ALU.add),
                             reads=[ut, vt_, self.Tcols], writes=[vt_])
                vba, vbt = vbr.next()
                c.op(c.pool, lambda: nc.gpsimd.tensor_copy(out=vba, in_=va), reads=[vt_], writes=[vbt])
                return va, vt_, vba, vbt, uga, ugt

            def stageS(j, d, pr):
                va, vt_, vba, vbt, uga, ugt = pr
                col = d * 8 + j
                ra, rt = rr.next()
                ia, it = ir.next()
                for (t0, n, v) in TBS:
                    pa, pt = psr.next()
                    self.mm(pa[:, :n], bd[:, d, 0, j, :], vba[:, t0:t0 + n], True, True, [Tbd, vbt], pt)
                    c.op(c.act, lambda: nc.scalar.activation(out=ra[:, t0:t0 + n], in_=pa[:, :n], func=AF.Sigmoid,
                                                             bias=cb[:, C_BR + col:C_BR + col + 1], scale=1.0), reads=[pt, self.Tcols], writes=[rt])
                    pa, pt = psr.next()
                    self.mm(pa[:, :n], bd[:, d, 1, j, :], vba[:, t0:t0 + n], True, True, [Tbd, vbt], pt)
                    c.op(c.act, lambda: nc.scalar.activation(out=ia[:, t0:t0 + n], in_=pa[:, :n], func=AF.Sigmoid,
                                                             bias=cb[:, C_BI + col:C_BI + col + 1], scale=1.0), reads=[pt, self.Tcols], writes=[it])
                return ra, rt, ia, it

            def stageE(j, d, pr, sres):
                va, vt_, vba, vbt, uga, ugt = pr
                ra, rt, ia, it = sres
                col = d * 8 + j
                aa, at = ar.next()
                ea, et = er.next()
                c.op(c.act, lambda: nc.scalar.activation(out=aa, in_=ra, func=AF.Exp, scale=self.nsp[:, col:col + 1]), reads=[rt, self.Tmod], writes=[at])
                c.op(c.pool, lambda: nc.gpsimd.tensor_tensor(out=ia, in0=ia, in1=va, op=ALU.mult), reads=[it, vt_], writes=[it])
                c.op(c.pool, lambda: nc.gpsimd.tensor_tensor(out=ea, in0=aa, in1=aa, op=ALU.mult), reads=[at], writes=[et])
                return aa, at, ea, et, ia, it

            def stageY(j, d, eres, ya, yt):
                aa, at, ea, et, ia, it = eres
                c.op(c.act, lambda: nc.scalar.activation(out=ea, in_=ea, func=AF.Sqrt, bias=1.0, scale=-1.0), reads=[et], writes=[et])
                c.op(c.dve, lambda: nc.vector.tensor_tensor(out=ia, in0=ia, in1=ea, op=ALU.mult), reads=[it, et], writes=[it])
                if d == 0:
                    ha, ht = ya, yt
                    c.op(c.dve, lambda: nc.vector.tensor_tensor_scan(out=ha[:, TL:NT], data0=aa[:, TL:NT], data1=ia[:, TL:NT], initial=0.0,
                                                                     op0=ALU.mult, op1=ALU.add), reads=[at, it], writes=[ht])
                    c.op(c.dve, lambda: nc.vector.tensor_tensor_scan(out=ha[:, 0:TL], data0=aa[:, 0:TL], data1=ia[:, 0:TL], initial=ha[:, NT - 1:NT],
                                                                     op0=ALU.mult, op1=ALU.add), reads=[at, it, ht], writes=[ht])
                else:
                    ha, ht = hr.next()
                    c.op(c.dve, lambda: nc.vector.tensor_tensor_scan(out=rev(ha, TL, TC), data0=rev(aa, TL, TC), data1=rev(ia, TL, TC), initial=0.0,
                                                                     op0=ALU.mult, op1=ALU.add), reads=[at, it], writes=[ht])
                    c.op(c.dve, lambda: nc.vector.tensor_tensor_scan(out=rev(ha, 0, TL), data0=rev(aa, 0, TL), data1=rev(ia, 0, TL), initial=ha[:, TL:TL + 1],
                                                                     op0=ALU.mult, op1=ALU.add), reads=[at, it, ht], writes=[ht])
                    c.op(c.pool, lambda: nc.gpsimd.tensor_tensor(out=ya, in0=ya, in1=ha, op=ALU.add), reads=[yt, ht], writes=[yt])

            def tail(j, pr, ya, yt):
                va, vt_, vba, vbt, uga, ugt = pr
                ga, gt = gr.next()
                c.op(c.act, lambda: nc.scalar.activation(out=ga, in_=uga, func=AF.Gelu_apprx_tanh), reads=[ugt], writes=[gt])
                sa, st = sr.next()
                c.op(c.dve, lambda: nc.vector.tensor_tensor(out=sa, in0=ya, in1=ga, op=ALU.mult), reads=[yt, gt], writes=[st])
                c.dma(c.sp, self.YR[j], sa, reads=[st])

            lds = {0: load(0), 1: load(1)}
            pr = prep(0, lds.pop(0))
            for j in range(8):
                if j + 2 < 8:
                    lds[j + 2] = load(j + 2)
                ya, yt = yr.next()
                s0 = stageS(j, 0, pr)
                s1 = stageS(j, 1, pr)
                e0 = stageE(j, 0, pr, s0)
                e1 = stageE(j, 1, pr, s1)
                stageY(j, 0, e0, ya, yt)
                pr_next = prep(j + 1, lds.pop(j + 1)) if j + 1 < 8 else None
                stageY(j, 1, e1, ya, yt)
                tail(j, pr, ya, yt)
                pr = pr_next
            c.barrier()

    def phase_fourier(self, l, do_ctx=True):
        nc, c = self.nc, self.c
        with ExitStack() as es:
            sbt = lambda name, shape, dt: es.enter_context(nc.sbuf_tensor(self.uniq(name), list(shape), dt)).ap()
            A = sbt("fA", [128, 8, 18, 256], BF16)
            At = [T(f"fA{g}") for g in range(8)]
            ufr = Ring([sbt(f"fu{i}", [128, NT], BF16) for i in range(2)], "fu")
            tabr = Ring([sbt(f"ftab{i}", [128, 2, 16, 512], BF16) for i in range(2)], "ftab")
            tabc = sbt("ftabc", [128, 2, 2, 256], BF16)
            Ttc = T("ftabc")
            stg = Ring([sbt(f"fstg{i}", [128, 512], BF16) for i in range(3)], "fstg")
            c.dma(c.sp, tabc, self.k_dftc.rearrange("a (tt p) u -> p a tt u", p=128), writes=[Ttc])
            csG = self.ksm[:, 256:512]
            ps1 = self.psring([0, 1, 2, 3])
            ps2 = self.psring([4, 5, 6, 7])
            ke = 0
            def preu(g):
                ua, ut = ufr.next()
                c.dma(c.sp, ua, self.U[16 + g], writes=[ut])
                return ua, ut

            nxtu = preu(0)
            for g in range(8):
                ua, ut = nxtu
                if g + 1 < 8:
                    nxtu = preu(g + 1)
                for tt in range(18):
                    pa, pt = ps1.next()
                    self.mm(pa[:, 0:256], ua[:, tt * 128:(tt + 1) * 128], csG, True, True, [ut, self.Tconst], pt)
                    if ke % 2 == 0:
                        c.op(c.act, lambda: nc.scalar.copy(out=A[:, g, tt, :], in_=pa[:, 0:256]), reads=[pt], writes=[At[g]])
                    else:
                        c.op(c.dve, lambda: nc.vector.tensor_copy(out=A[:, g, tt, :], in_=pa[:, 0:256]), reads=[pt], writes=[At[g]])
                    ke += 1
            dv = self.k_dft.rearrange("a (tt p) u -> p a tt u", p=128)
            def pret(tb):
                ta, tt_ = tabr.next()
                c.dma(c.sp, ta[:, 0], dv[:, 0, :, tb * 512:(tb + 1) * 512], writes=[tt_])
                c.dma(c.sp, ta[:, 1], dv[:, 1, :, tb * 512:(tb + 1) * 512], writes=[tt_])
                return ta, tt_

            nxtt = pret(0)
            for tb in range(4):
                ta, tt_ = nxtt
                if tb + 1 < 4:
                    nxtt = pret(tb + 1)
                for g in range(8):
                    pa, pt = ps2.next()
                    for tt in range(16):
                        self.mm(pa, A[:, g, tt, 0:128], ta[:, 0, tt, :], tt == 0, False, [At[g], tt_], pt, sig=False)
                        self.mm(pa, A[:, g, tt, 128:256], ta[:, 1, tt, :], False, tt == 15, [At[g], tt_], pt)
                    sa, st = stg.next()
                    c.op(c.act, lambda: nc.scalar.mul(out=sa, in_=pa, mul=1.0 / 512.0), reads=[pt], writes=[st])
                    c.dma(c.sp, self.YF[g][:, tb * 512:(tb + 1) * 512], sa, reads=[st])
            sc_c = 1.0 / np.sqrt(TC * 128.0)
            for g in (range(8) if do_ctx else ()):
                pa, pt = ps2.next()
                for tt in range(2):
                    self.mm(pa[:, 0:256], A[:, g, 16 + tt, 0:128], tabc[:, 0, tt, :], tt == 0, False, [At[g], Ttc], pt, sig=False)
                    self.mm(pa[:, 0:256], A[:, g, 16 + tt, 128:256], tabc[:, 1, tt, :], False, tt == 1, [At[g], Ttc], pt)
                sa, st = stg.next()
                c.op(c.act, lambda: nc.scalar.mul(out=sa[:, 0:256], in_=pa[:, 0:256], mul=float(sc_c)), reads=[pt], writes=[st])
                c.dma(c.sp, self.YF[g][:, TL:NT], sa[:, 0:256], reads=[st])
            c.barrier()

    def phase_merge(self, l, do_ctx=True):
        nc, c = self.nc, self.c
        with ExitStack() as es:
            sbt = lambda name, shape, dt: es.enter_context(nc.sbuf_tensor(self.uniq(name), list(shape), dt)).ap()
            ys = [sbt(f"my{k}", [128, 8, NT], BF16) for k in range(3)]
            Ty = [T(f"my{k}") for k in range(3)]
            for k, src in enumerate((self.YA, self.YR, self.YF)):
                c.dma(c.sp, ys[k], src.rearrange("f p t -> p f t"), writes=[Ty[k]])
            wr = Ring([sbt(f"mw{i}", [128, 3, 8, 128], BF16) for i in range(3)], "mw")
            gr = Ring([sbt(f"mg{i}", [128, 3, NT], BF16) for i in range(2)], "mg")
            stg = Ring([sbt(f"mstg{i}", [128, NT], BF16) for i in range(2)], "mstg")
            tr = [Ring([sbt(f"mt{k}_{i}", [128, 512], F32) for i in range(2)], f"mt{k}") for k in range(3)]
            psr = self.psring([0, 1, 2, 3, 4, 5, 6, 7])
            wbv = self.w_branch[l]
            Gv = self.G.rearrange("(k dt) p t -> dt p k t", k=3)
            def pre(dt_):
                wa, wt = wr.next()
                c.dma(c.pool, wa, wbv[dt_], writes=[wt])
                ga, gt = gr.next()
                c.dma(c.sp, ga, Gv[dt_], writes=[gt])
                return wa, wt, ga, gt

            nxt = pre(0)
            for dt_ in range(NJ):
                wa, wt, ga, gt = nxt
                if dt_ + 1 < NJ:
                    nxt = pre(dt_ + 1)
                sa, st = stg.next()
                for (t0, n, v) in (TBS if do_ctx else TBS[:4]):
                    tk = []
                    for k in range(3):
                        pa, pt = psr.next()
                        for kc in range(8):
                            self.mm(pa[:, :n], wa[:, k, kc, :], ys[k][:, kc, t0:t0 + n], kc == 0, kc == 7, [wt, Ty[k]], pt)
                        ta, tt_ = tr[k].next()
                        c.op(c.dve, lambda: nc.vector.tensor_tensor(out=ta[:, :n], in0=pa[:, :n], in1=ga[:, k, t0:t0 + n], op=ALU.mult),
                             reads=[pt, gt], writes=[tt_])
                        tk.append((ta, tt_))
                    c.op(c.pool, lambda: nc.gpsimd.tensor_tensor(out=tk[0][0][:, :n], in0=tk[0][0][:, :n], in1=tk[1][0][:, :n], op=ALU.add),
                         reads=[tk[0][1], tk[1][1]], writes=[tk[0][1]])
                    c.op(c.pool, lambda: nc.gpsimd.tensor_tensor(out=sa[:, t0:t0 + n], in0=tk[0][0][:, :n], in1=tk[2][0][:, :n], op=ALU.add),
                         reads=[tk[0][1], tk[2][1]], writes=[st])
                c.dma(c.sp, self.MRG[dt_], sa, reads=[st])
            c.barrier()

    def proj_residual(self, src, nk, wview, gate_off, halves, side_mod=None):
        nc, c = self.nc, self.c
        for hi, blocks in enumerate(halves):
            h0 = blocks[0][0]
            hn = blocks[-1][0] + blocks[-1][1] - h0
            with ExitStack() as es:
                sbt = lambda name, shape, dt: es.enter_context(nc.sbuf_tensor(self.uniq(name), list(shape), dt)).ap()
                a = sbt("pa_act", [128, nk, hn], BF16)
                Ta = T("pa_act")
                srcv = src.rearrange("f p t -> p f t")
                half_k = nk // 2
                c.dma(c.sp, a[:, :half_k, :], srcv[:, :half_k, h0:h0 + hn], writes=[Ta])
                c.dma(c.sp, a[:, half_k:, :], srcv[:, half_k:, h0:h0 + hn], writes=[Ta])
                wr = Ring([sbt(f"pw{i}", [128, nk, 128], BF16) for i in range(3)], "pw")
                xr = Ring([sbt(f"px{i}", [128, hn], F32) for i in range(3)], "px")
                psr = self.psring([0, 1, 2, 3, 4, 5, 6])
                if side_mod is not None:
                    emit_load, emit_mm = self.mod_emitter(side_mod, es)

                def pre(dt_):
                    wa, wt = wr.next()
                    c.dma(c.pool, wa, wview[dt_], writes=[wt])
                    xa, xt = xr.next()
                    c.dma(c.sp, xa, self.XT[dt_][:, h0:h0 + hn], writes=[xt])
                    return wa, wt, xa, xt

                nxt = pre(0)
                for dt_ in range(NJ):
                    wa, wt, xa, xt = nxt
                    if dt_ + 1 < NJ:
                        nxt = pre(dt_ + 1)
                    if side_mod is not None:
                        it = hi * NJ + dt_
                        g_lo, g_hi = (it * 3) // 2, ((it + 1) * 3) // 2
                        if dt_ == 0:
                            pend = []
                        for g in pend:
                            emit_mm(g)
                        for g in range(g_lo, g_hi):
                            emit_load(g)
                        pend = list(range(g_lo, g_hi))
                        if dt_ == NJ - 1:
                            for g in pend:
                                emit_mm(g)
                            pend = []
                    for (t0, n, v) in blocks:
                        pa, pt = psr.next()
                        for kc in range(nk):
                            self.mm(pa[:, :n], wa[:, kc, :], a[:, kc, t0 - h0:t0 - h0 + n], kc == 0, kc == nk - 1, [wt, Ta], pt)
                        c.op(c.dve, lambda: nc.vector.scalar_tensor_tensor(out=xa[:, t0 - h0:t0 - h0 + n], in0=pa[:, :n],
                                                                           scalar=self.mod[:, gate_off + dt_, v:v + 1],
                                                                           in1=xa[:, t0 - h0:t0 - h0 + n], op0=ALU.mult, op1=ALU.add),
                             reads=[pt, xt, self.Tmod], writes=[xt])
                    c.dma(c.sp, self.XT[dt_][:, h0:h0 + hn], xa, reads=[xt])
                c.barrier()

    def phase_ffn_in(self, l, side_mod=None, do_ctx=True):
        nc, c = self.nc, self.c
        with ExitStack() as es:
            sbt = lambda name, shape, dt: es.enter_context(nc.sbuf_tensor(self.uniq(name), list(shape), dt)).ap()
            h = sbt("h2", [128, NJ, NT], BF16)
            hts = [[T(f"h2{i}_{j}") for j in range(NJ)] for i in range(5)]
            wr = Ring([sbt(f"fw{i}", [128, NJ, 512], BF16) for i in range(3)], "fw")
            stg = Ring([sbt(f"fs{i}", [128, NT], BF16) for i in range(2)], "fs")
            sg = Ring([sbt(f"fsg{i}", [128, 512], F32) for i in range(2)], "fsg")
            GW = 128
            NG = 12288 // GW
            if side_mod is not None:
                emit_load, emit_mm = self.mod_emitter(side_mod, es, gw=GW, cast_on_pool=True)
            with ExitStack() as es2:
                self.norm_to_h(es2, self.A2, 48, h, hts, bw=256, nbuf=2, do_ctx=do_ctx)
            psg = self.psring([0, 1, 2, 3])
            psu = self.psring([4, 5, 6])
            wv = self.w_ffn_in[l].rearrange("(kc p) f -> p kc f", p=128)
            pend = []
            for grp in range(11):
                wga, wgt = wr.next()
                c.dma(c.pool, wga, wv[:, :, grp * 512:(grp + 1) * 512], writes=[wgt])
                wua, wut = wr.next()
                c.dma(c.pool, wua, wv[:, :, DFF + grp * 512:DFF + (grp + 1) * 512], writes=[wut])
                for ft in range(4):
                    f = grp * 4 + ft
                    sa, st = stg.next()
                    for bi, (t0, n, v) in enumerate(TBS if do_ctx else TBS[:4]):
                        if side_mod is not None:
                            tick = f * 5 + bi
                            if tick % 2 == 0 and tick // 2 < NG:
                                for g in pend:
                                    emit_mm(g)
                                emit_load(tick // 2)
                                pend = [tick // 2]
                        pg, pgt = psg.next()
                        for kc in range(NJ):
                            self.mm(pg[:, :n], wga[:, kc, ft * 128:(ft + 1) * 128], h[:, kc, t0:t0 + n], kc == 0, kc == NJ - 1, [hts[bi][kc], wgt], pgt)
                        pu, put = psu.next()
                        for kc in range(NJ):
                            self.mm(pu[:, :n], wua[:, kc, ft * 128:(ft + 1) * 128], h[:, kc, t0:t0 + n], kc == 0, kc == NJ - 1, [hts[bi][kc], wut], put)
                        ga, gt = sg.next()
                        c.op(c.act, lambda: nc.scalar.activation(out=ga[:, :n], in_=pg[:, :n], func=AF.Silu), reads=[pgt], writes=[gt])
                        c.op(c.dve, lambda: nc.vector.tensor_tensor(out=sa[:, t0:t0 + n], in0=pu[:, :n], in1=ga[:, :n], op=ALU.mult),
                             reads=[put, gt], writes=[st])
                    c.dma(c.sp, self.HID[f], sa, reads=[st])
            for g in pend:
                emit_mm(g)
            c.barrier()

    def phase_final(self):
        nc, c = self.nc, self.c
        with ExitStack() as es:
            sbt = lambda name, shape, dt: es.enter_context(nc.sbuf_tensor(self.uniq(name), list(shape), dt)).ap()
            xb = Ring([sbt(f"zxb{i}", [128, NJ, 512], F32) for i in range(2)], "zxb")
            sq = Ring([sbt("zsq0", [128, NJ, 512], F32)], "zsq")
            rs = Ring([sbt(f"zrs{i}", [128, 512], F32) for i in range(2)], "zrs")
            orow = Ring([sbt(f"zo{i}", [128, D], F32) for i in range(2)], "zo")
            psr = self.psring([6, 7])
            pst = self.psring([0, 1, 2, 3])
            XTv = self.XT.rearrange("j p t -> p j t")
            cb = self.colsb
            k = 0
            for (t0, n, v) in TBS[:4]:
                xa, xt = xb.next()
                c.dma(c.sp, xa, XTv[:, :, t0:t0 + n], writes=[xt])
                sa, st = sq.next()
                c.op(c.act, lambda: nc.scalar.activation(out=sa, in_=xa, func=AF.Square), reads=[xt], writes=[st])
                pa, pt = psr.next()
                for j in range(NJ):
                    self.mm(pa, self.ones32, sa[:, j, :], j == 0, j == NJ - 1, [st, self.Tconst], pt)
                ra, rt = rs.next()
                self.rstd_from_ps(pa, pt, ra, rt, 1.0 / D)
                for j in range(NJ):
                    c.op(c.dve, lambda: nc.vector.scalar_tensor_tensor(out=xa[:, j, :], in0=xa[:, j, :], scalar=cb[:, C_GF + j:C_GF + j + 1], in1=ra,
                                                                       op0=ALU.mult, op1=ALU.mult), reads=[xt, rt, self.Tcols], writes=[xt])
                for tt in range(4):
                    oa, ot = orow.next()
                    for j4 in range(4):
                        qa, qt = pst.next()
                        for jj in range(4):
                            j = j4 * 4 + jj
                            c.op(c.pe, lambda: nc.tensor.transpose(qa[:, jj * 128:(jj + 1) * 128], xa[:, j, tt * 128:(tt + 1) * 128], self.ident32),
                                 reads=[xt, self.Tconst], writes=[qt], sig=(jj == 3))
                        if k % 2 == 0:
                            c.op(c.act, lambda: nc.scalar.copy(out=oa[:, j4 * 512:(j4 + 1) * 512], in_=qa), reads=[qt], writes=[ot])
                        else:
                            c.op(c.dve, lambda: nc.vector.tensor_copy(out=oa[:, j4 * 512:(j4 + 1) * 512], in_=qa), reads=[qt], writes=[ot])
                        k += 1
                    r0 = t0 + tt * 128
                    c.dma(c.sp, self.out[r0:r0 + 128, :], oa, reads=[ot])
            c.barrier()

    def build(self, upto=None):
        self.phase_init()
        for l in range(self.n_layers):
            last = (l == self.n_layers - 1)
            dc = not last
            halves = HALF_BLOCKS if dc else HALF_BLOCKS_LAST
            self.phase_mod(l, standalone=(l == 0))
            self.phase_win(l)
            self.phase_attn(l, do_ctx=dc)
            self.phase_lru(l)
            self.phase_fourier(l, do_ctx=dc)
            self.phase_merge(l, do_ctx=dc)
            self.proj_residual(self.MRG, NJ, self.w_out[l], 32, halves)
            self.phase_ffn_in(l, side_mod=(l + 1 if not last else None), do_ctx=dc)
            self.proj_residual(self.HID, 44, self.w_ffn_out[l], 80, halves, side_mod=None)
        self.phase_final()
        return self.nc


def _col(v):
    v = np.asarray(v, np.float32)
    return np.ascontiguousarray(v.reshape(-1, 128).T)


def make_consts():
    bf = ml_dtypes.bfloat16
    t = np.arange(TL)
    row = (t // 64).astype(np.float64)
    colp = (t % 64).astype(np.float64)
    inv = 10000.0 ** (-np.arange(32, dtype=np.float64) / 32)
    ang = np.concatenate([row[:, None] * inv, colp[:, None] * inv], axis=-1)
    angT = np.concatenate([ang.T, ang.T], axis=0)
    k_rope = np.stack([np.cos(angT), np.sin(angT)]).astype(np.float32).astype(bf)
    ident = np.eye(128, dtype=np.float32)
    rotT = np.zeros((128, 128), np.float32)
    for m in range(64):
        rotT[m + 64, m] = -1.0
        rotT[m, m + 64] = 1.0
    cc = np.arange(128)
    ph = 2 * np.pi * ((cc[:, None] * cc[None, :]) % 128) / 128.0
    k_small = np.concatenate([ident, rotT, np.cos(ph), np.sin(ph)], axis=1).astype(np.float32).astype(bf)
    th = 2 * np.pi * ((t[:, None].astype(np.int64) * t[None, :]) % TL) / float(TL)
    k_dft = np.stack([np.cos(th), -np.sin(th)]).astype(np.float32).astype(bf)
    tc = np.arange(TC)
    thc = 2 * np.pi * ((tc[:, None] * tc[None, :]) % TC) / float(TC)
    k_dftc = np.stack([np.cos(thc), -np.sin(thc)]).astype(np.float32).astype(bf)
    return dict(k_rope=k_rope, k_small=k_small, k_ident32=ident, k_dft=k_dft, k_dftc=k_dftc)


def make_cols(inp):
    cols = np.zeros((DEPTH, 128, NCOL), np.float32)
    for l in range(DEPTH):
        cols[l, :, C_BMOD:C_BMOD + 96] = _col(inp["b_mod"][l])
        cols[l, :, C_GN1:C_GN1 + 16] = _col(inp["g_norm1"][l])
        cols[l, :, C_GN2:C_GN2 + 16] = _col(inp["g_norm2"][l])
        cols[l, :, C_BG:C_BG + 48] = _col(inp["b_gate"][l])
        for tap in range(4):
            cols[l, :, C_CW + tap * 8:C_CW + tap * 8 + 8] = _col(inp["conv_w"][l, tap])
        cols[l, :, C_CB:C_CB + 8] = _col(inp["conv_b"][l])
        for d in range(2):
            cols[l, :, C_BR + d * 8:C_BR + d * 8 + 8] = _col(inp["lru_b_r"][l, d])
            cols[l, :, C_BI + d * 8:C_BI + d * 8 + 8] = _col(inp["lru_b_i"][l, d])
            cols[l, :, C_LAM + d * 8:C_LAM + d * 8 + 8] = _col(inp["lru_lambda"][l, d])
        cols[l, :, C_QG] = inp["q_gain"][l]
        cols[l, :, C_KG] = inp["k_gain"][l]
        cols[l, :, C_GF:C_GF + 16] = _col(inp["g_final"])
    return cols


def make_in_maps(inp, cores):
    inp = {k: np.asarray(v) for k, v in inp.items()}
    consts = make_consts()
    cols = make_cols(inp)
    shared = dict(cols=cols, **consts)
    for k in ("w_mod", "w_in", "lru_w_r", "lru_w_i", "w_gate", "w_ffn_in"):
        shared[k] = np.ascontiguousarray(inp[k], dtype=np.float32)
    shared["w_out"] = np.ascontiguousarray(
        np.asarray(inp["w_out"], np.float32).reshape(DEPTH, NJ, 128, NJ, 128).transpose(0, 3, 2, 1, 4))
    shared["w_ffn_out"] = np.ascontiguousarray(
        np.asarray(inp["w_ffn_out"], np.float32).reshape(DEPTH, 44, 128, NJ, 128).transpose(0, 3, 2, 1, 4))
    shared["w_branch"] = np.ascontiguousarray(
        np.asarray(inp["w_branch"], np.float32).reshape(DEPTH, 3, 8, 128, NJ, 128).transpose(0, 4, 3, 1, 2, 5))
    maps = []
    for b in cores:
        cvec = np.stack([_col(inp["c"][b]), _col(inp["c_ctx"])], axis=-1)
        m = dict(shared)
        m["x"] = np.ascontiguousarray(inp["x"][b], dtype=np.float32)
        m["ctx"] = np.ascontiguousarray(inp["ctx"][b], dtype=np.float32)
        m["cvec"] = np.ascontiguousarray(cvec, dtype=np.float32)
        maps.append(m)
    return maps


def kernel(**inputs):
    n = 8
    nc = Builder(DEPTH).build()
    in_maps = make_in_maps(inputs, list(range(n)))
    res = run_bass_kernel_spmd(nc, in_maps, core_ids=list(range(n)))
    return np.stack([np.asarray(r["out"], dtype=np.float32) for r in res.results], axis=0)
```

```python
import numpy as np
import ml_dtypes
import concourse.bass as bass
import concourse.mybir as mybir
from contextlib import ExitStack
from concourse.bass_utils import run_bass_kernel_spmd

F32 = mybir.dt.float32
BF16 = mybir.dt.bfloat16
AF = mybir.ActivationFunctionType
ALU = mybir.AluOpType

D = 2048
TL = 2048
TC = 256
NT = TL + TC
DEPTH = 4
DIN = 4608
DFF = 5632
NJ = 16
EPS = 1e-6
SEM_ROT = 30000

TBS = [(0, 512, 0), (512, 512, 0), (1024, 512, 0), (1536, 512, 0), (2048, 256, 1)]
HALF_BLOCKS_LAST = [[(0, 512, 0), (512, 512, 0)], [(1024, 512, 0), (1536, 512, 0)]]
HALF_BLOCKS = [[(0, 512, 0), (512, 512, 0), (1024, 128, 0)], [(1152, 512, 0), (1664, 384, 0), (2048, 256, 1)]]

C_BMOD = 0
C_GN1 = 96
C_GN2 = 112
C_BG = 128
C_CW = 176
C_CB = 208
C_BR = 216
C_BI = 232
C_LAM = 248
C_QG = 264
C_KG = 265
C_GF = 266
NCOL = 282


class Tok:
    __slots__ = ("sem", "val", "eng")

    def __init__(self, sem=None, val=None, eng=None):
        self.sem = sem
        self.val = val
        self.eng = eng


class T:
    __slots__ = ("name", "w", "r")

    def __init__(self, name):
        self.name = name
        self.w = None
        self.r = []


class Eng:
    def __init__(self, ctx, name, h):
        self.ctx = ctx
        self.name = name
        self.h = h
        self.sem = None
        self.cnt = 0
        self.nsem = 0
        self.waited = {}
        self.pending = []
        self.last = None
        self.n_ins = 0
        self.n_wait = 0

    def cur_sem(self):
        if self.sem is None or self.cnt >= SEM_ROT:
            self.sem = self.ctx.nc.alloc_semaphore(f"s_{self.name}_{self.nsem}")
            self.nsem += 1
            self.cnt = 0
        return self.sem


class Ctx:
    def __init__(self, nc, n_dma_sems=10):
        self.nc = nc
        self.pe = Eng(self, "pe", nc.tensor)
        self.act = Eng(self, "act", nc.scalar)
        self.dve = Eng(self, "dve", nc.vector)
        self.pool = Eng(self, "pool", nc.gpsimd)
        self.sp = Eng(self, "sp", nc.sync)
        self.engs = [self.pe, self.act, self.dve, self.pool, self.sp]
        self.dsems = {}
        for e in (self.sp, self.pool):
            self.dsems[e.name] = [[nc.alloc_semaphore(f"d_{e.name}_{i}"), 0, None] for i in range(n_dma_sems)]
        self.drr = {e.name: 0 for e in (self.sp, self.pool)}
        self.n_dma = 0

    def _wait(self, eng, tok):
        if tok is None:
            return
        if tok.eng is eng and eng is self.pe:
            return
        if tok.sem is None:
            raise RuntimeError(f"unresolved dependency needed by {eng.name}")
        key = tok.sem.name
        if eng.waited.get(key, 0) >= tok.val:
            return
        eng.h.wait_ge(tok.sem, tok.val)
        eng.waited[key] = tok.val
        eng.n_wait += 1

    def _deps(self, eng, reads, writes):
        for t in reads:
            self._wait(eng, t.w)
        for t in writes:
            self._wait(eng, t.w)
            for r in t.r:
                self._wait(eng, r)

    def _mark(self, tok, reads, writes):
        for t in reads:
            t.r = [r for r in t.r if not (r.sem is not None and tok.sem is not None and r.sem is tok.sem and r.val <= tok.val)]
            t.r.append(tok)
        for t in writes:
            t.w = tok
            t.r = []

    def op(self, eng, fn, reads=(), writes=(), sig=True):
        self._deps(eng, reads, writes)
        ins = fn()
        eng.n_ins += 1
        tok = Tok(eng=eng)
        if sig:
            sem = eng.cur_sem()
            ins.then_inc(sem, 1)
            eng.cnt += 1
            tok.sem = sem
            tok.val = eng.cnt
            for p in eng.pending:
                p.sem = sem
                p.val = eng.cnt
            eng.pending = []
            eng.last = tok
        else:
            eng.pending.append(tok)
        self._mark(tok, reads, writes)
        return tok

    def dma(self, q, out, in_, reads=(), writes=()):
        self._deps(q, reads, writes)
        lst = self.dsems[q.name]
        i = self.drr[q.name]
        self.drr[q.name] = (i + 1) % len(lst)
        slot = lst[i]
        if slot[2] is not None:
            self._wait(q, slot[2])
        ins = q.h.dma_start(out=out, in_=in_)
        slot[1] += 16
        ins.then_inc(slot[0], 16)
        tok = Tok(sem=slot[0], val=slot[1], eng=None)
        slot[2] = tok
        self.n_dma += 1
        self._mark(tok, reads, writes)
        return tok

    def barrier(self):
        toks = []
        for e in self.engs:
            assert not e.pending, f"pending nosig ops on {e.name} at barrier"
            if e.last is not None:
                toks.append(e.last)
        for lst in self.dsems.values():
            for slot in lst:
                if slot[2] is not None:
                    toks.append(slot[2])
        for e in self.engs:
            for t in toks:
                if t.eng is e and e is self.pe:
                    continue
                self._wait(e, t)

    def stats(self):
        d = {e.name: (e.n_ins, e.n_wait, e.nsem) for e in self.engs}
        d["dma"] = self.n_dma
        return d


class Ring:
    def __init__(self, aps, name):
        self.aps = aps
        self.ts = [T(f"{name}{i}") for i in range(len(aps))]
        self.i = 0

    def next(self):
        k = self.i % len(self.aps)
        self.i += 1
        return self.aps[k], self.ts[k]


class Builder:
    def __init__(self, n_layers=DEPTH, dbg=()):
        self.n_layers = n_layers
        self.dbg = set(dbg)
        nc = self.nc = bass.Bass("TRN2", target_bir_lowering=False)
        self.c = Ctx(nc)
        inp = lambda name, shape, dt=F32: nc.dram_tensor(name, list(shape), dt, kind="ExternalInput").ap()
        self.x = inp("x", [TL, D])
        self.ctx_in = inp("ctx", [TC, D])
        self.cvec = inp("cvec", [128, NJ, 2])
        self.cols = inp("cols", [DEPTH, 128, NCOL])
        self.w_mod = inp("w_mod", [DEPTH, 96, 128, NJ, 128])
        self.w_in = inp("w_in", [DEPTH, D, DIN])
        self.lru_w_r = inp("lru_w_r", [DEPTH, 2, 16, 64, 64])
        self.lru_w_i = inp("lru_w_i", [DEPTH, 2, 16, 64, 64])
        self.w_branch = inp("w_branch", [DEPTH, NJ, 128, 3, 8, 128])
        self.w_gate = inp("w_gate", [DEPTH, D, 3 * D])
        self.w_out = inp("w_out", [DEPTH, NJ, 128, NJ, 128])
        self.w_ffn_in = inp("w_ffn_in", [DEPTH, D, 2 * DFF])
        self.w_ffn_out = inp("w_ffn_out", [DEPTH, NJ, 128, 44, 128])
        self.k_rope = inp("k_rope", [2, 128, TL], BF16)
        self.k_small = inp("k_small", [128, 128 * 2 + 256], BF16)
        self.k_ident32 = inp("k_ident32", [128, 128])
        self.k_dft = inp("k_dft", [2, TL, TL], BF16)
        self.k_dftc = inp("k_dftc", [2, TC, TC], BF16)
        self.out = nc.dram_tensor("out", [TL, D], F32, kind="ExternalOutput").ap()
        self.XT = self.scr("XT", [NJ, 128, NT], F32)
        self.QK = self.scr("QK", [10, 128, NT], BF16)
        self.VT = self.scr("VT", [18, 128, 256], BF16)
        self.U = self.scr("U", [24, 128, NT], BF16)
        self.G = self.scr("G", [48, 128, NT], BF16)
        self.YA = self.scr("YA", [8, 128, NT], BF16)
        self.YR = self.scr("YR", [8, 128, NT], BF16)
        self.YF = self.scr("YF", [8, 128, NT], BF16)
        self.MRG = self.scr("MRG", [NJ, 128, NT], BF16)
        self.HID = self.scr("HID", [44, 128, NT], BF16)
        self.ps = [nc.alloc_psum_tensor(f"ps{i}", [128, 512], F32).ap() for i in range(8)]
        self.pst = [T(f"ps{i}") for i in range(8)]
        sb = lambda name, shape, dt: nc.alloc_sbuf_tensor(name, list(shape), dt).ap()
        self.ident32 = sb("ident32", [128, 128], F32)
        self.ones32 = sb("ones32", [128, 128], F32)
        self.ksm = sb("ksm", [128, 512], BF16)
        self.ones_bf = sb("ones_bf", [128, 128], BF16)
        self.epsc = sb("epsc", [128, 1], F32)
        self.colsb = sb("colsb", [128, NCOL], F32)
        self.mod = sb("mod", [128, 96, 2], F32)
        self.A1 = sb("A1", [128, NJ, 2], F32)
        self.A2 = sb("A2", [128, NJ, 2], F32)
        self.nsp = sb("nsp", [128, 16], F32)
        self.nsp2 = sb("nsp2", [128, 16], F32)
        self.cs = sb("cs", [128, NJ, 2], BF16)
        self.cv32 = sb("cv32", [128, NJ, 2], F32)
        self.Tconst = T("const")
        self.Tcols = T("cols")
        self.Tmod = T("mod")

    def uniq(self, name):
        self._uid = getattr(self, "_uid", 0) + 1
        return f"{name}_{self._uid}"

    def scr(self, name, shape, dt):
        kind = "ExternalOutput" if name in self.dbg else "Internal"
        return self.nc.dram_tensor("scr_" + name, list(shape), dt, kind=kind).ap()

    def psring(self, idxs, name="psr"):
        r = Ring([self.ps[i] for i in idxs], name)
        r.ts = [self.pst[i] for i in idxs]
        return r

    def mm(self, out, lhsT, rhs, start, stop, reads, wt, sig=None):
        nc = self.nc
        if sig is None:
            sig = stop
        return self.c.op(self.c.pe, lambda: nc.tensor.matmul(out, lhsT=lhsT, rhs=rhs, start=start, stop=stop),
                         reads=reads, writes=[wt], sig=sig)

    def rstd_from_ps(self, ps_ap, ps_t, out_ap, out_t, scale):
        nc, c = self.nc, self.c
        c.op(c.act, lambda: nc.scalar.activation(out=out_ap, in_=ps_ap, func=AF.Sqrt, bias=self.epsc[:, 0:1], scale=scale),
             reads=[ps_t, self.Tconst], writes=[out_t])
        c.op(c.dve, lambda: nc.vector.reciprocal(out=out_ap, in_=out_ap), reads=[out_t], writes=[out_t])

    def phase_init(self):
        nc, c = self.nc, self.c
        with ExitStack() as es:
            sbt = lambda name, shape, dt: es.enter_context(nc.sbuf_tensor(self.uniq(name), list(shape), dt)).ap()
            c.dma(c.sp, self.ident32, self.k_ident32, writes=[self.Tconst])
            c.dma(c.sp, self.ksm, self.k_small, writes=[self.Tconst])
            c.op(c.dve, lambda: nc.vector.memset(self.ones32, 1.0), writes=[self.Tconst])
            c.op(c.dve, lambda: nc.vector.memset(self.ones_bf, 1.0), writes=[self.Tconst])
            c.op(c.dve, lambda: nc.vector.memset(self.epsc, EPS), writes=[self.Tconst])
            c.dma(c.sp, self.cv32, self.cvec, writes=[self.Tconst])
            c.op(c.act, lambda: nc.scalar.activation(out=self.cs, in_=self.cv32, func=AF.Silu), reads=[self.Tconst], writes=[self.Tconst])
            xin = Ring([sbt(f"xin{i}", [128, D], F32) for i in range(2)], "xin")
            xo = Ring([sbt(f"xo{i}", [128, NJ, 128], F32) for i in range(2)], "xo")
            psr = self.psring([0, 1, 2, 3])
            XTv = self.XT.rearrange("j p t -> p j t")
            k = 0
            for tt in range(18):
                src = self.x[tt * 128:(tt + 1) * 128, :] if tt < 16 else self.ctx_in[(tt - 16) * 128:(tt - 15) * 128, :]
                xa, xt = xin.next()
                c.dma(c.sp, xa, src, writes=[xt])
                oa, ot = xo.next()
                for j4 in range(4):
                    pa, pt = psr.next()
                    for jj in range(4):
                        j = j4 * 4 + jj
                        c.op(c.pe, lambda: nc.tensor.transpose(pa[:, jj * 128:(jj + 1) * 128], xa[:, j * 128:(j + 1) * 128], self.ident32),
                             reads=[xt, self.Tconst], writes=[pt], sig=(jj == 3))
                    dst = oa[:, j4 * 4:(j4 + 1) * 4, :]
                    src_ps = pa.rearrange("p (a b) -> p a b", b=128)
                    if k % 2 == 0:
                        c.op(c.act, lambda: nc.scalar.copy(out=dst, in_=src_ps), reads=[pt], writes=[ot])
                    else:
                        c.op(c.dve, lambda: nc.vector.tensor_copy(out=dst, in_=src_ps), reads=[pt], writes=[ot])
                    k += 1
                c.dma(c.sp, XTv[:, :, tt * 128:(tt + 1) * 128], oa, reads=[ot])
            c.barrier()

    def mod_emitter(self, l, es, gw=256):
        nc, c = self.nc, self.c
        sbt = lambda name, shape, dt: es.enter_context(nc.sbuf_tensor(self.uniq(name), list(shape), dt)).ap()
        f32r = Ring([sbt(f"wmf{i}", [128, NJ, gw], F32) for i in range(2)], "wmf")
        bfr = Ring([sbt(f"wmb{i}", [128, NJ, gw], BF16) for i in range(2)], "wmb")
        assert gw == 128
        wv = self.w_mod[l]
        pm, pmt = self.ps[7], self.pst[7]
        stash = {}

        def emit_load(g):
            fa, ft_ = f32r.next()
            c.dma(c.sp, fa, wv[g], writes=[ft_])
            ba, bt = bfr.next()
            c.op(c.act, lambda: nc.scalar.copy(out=ba, in_=fa), reads=[ft_], writes=[bt])
            stash[g] = (ba, bt)

        def emit_mm(g):
            ba, bt = stash.pop(g)
            nft = gw // 128
            for ft in range(nft):
                f = g * nft + ft
                for kc in range(NJ):
                    self.mm(pm[:, f * 2:(f + 1) * 2], ba[:, kc, ft * 128:(ft + 1) * 128], self.cs[:, kc, :],
                            kc == 0, kc == NJ - 1, [bt, self.Tconst], pmt, sig=(kc == NJ - 1 and ft == nft - 1))

        return emit_load, emit_mm

    def phase_mod(self, l, standalone):
        nc, c = self.nc, self.c
        with ExitStack() as es:
            c.dma(c.sp, self.colsb, self.cols[l], writes=[self.Tcols])
            pm, pmt = self.ps[7], self.pst[7]
            if standalone:
                emit_load, emit_mm = self.mod_emitter(l, es, gw=128)
                for g in range(96):
                    emit_load(g)
                    if g > 0:
                        emit_mm(g - 1)
                emit_mm(95)
            pmv = pm[:, 0:192].rearrange("p (f v) -> p f v", v=2)
            cb = self.colsb
            for v in range(2):
                c.op(c.dve, lambda: nc.vector.tensor_tensor(out=self.mod[:, :, v], in0=pmv[:, :, v], in1=cb[:, C_BMOD:C_BMOD + 96], op=ALU.add),
                     reads=[pmt, self.Tcols], writes=[self.Tmod])
            for v in range(2):
                c.op(c.dve, lambda: nc.vector.scalar_tensor_tensor(out=self.A1[:, :, v], in0=self.mod[:, 16:32, v], scalar=1.0,
                                                                   in1=cb[:, C_GN1:C_GN1 + 16], op0=ALU.add, op1=ALU.mult),
                     reads=[self.Tmod, self.Tcols], writes=[self.Tmod])
                c.op(c.dve, lambda: nc.vector.scalar_tensor_tensor(out=self.A2[:, :, v], in0=self.mod[:, 64:80, v], scalar=1.0,
                                                                   in1=cb[:, C_GN2:C_GN2 + 16], op0=ALU.add, op1=ALU.mult),
                     reads=[self.Tmod, self.Tcols], writes=[self.Tmod])
            c.op(c.act, lambda: nc.scalar.activation(out=self.nsp, in_=cb[:, C_LAM:C_LAM + 16], func=AF.Exp, scale=-1.0),
                 reads=[self.Tcols], writes=[self.Tmod])
            c.op(c.act, lambda: nc.scalar.activation(out=self.nsp, in_=self.nsp, func=AF.Ln, bias=1.0, scale=1.0),
                 reads=[self.Tmod], writes=[self.Tmod])
            c.op(c.dve, lambda: nc.vector.tensor_scalar_mul(out=self.nsp2, in0=self.nsp, scalar1=-16.0), reads=[self.Tmod], writes=[self.Tmod])
            c.op(c.dve, lambda: nc.vector.tensor_scalar_mul(out=self.nsp, in0=self.nsp, scalar1=-8.0), reads=[self.Tmod], writes=[self.Tmod])
            c.barrier()

    def norm_to_h(self, es, A, Boff, h, hts, bw=512, nbuf=3, do_ctx=True):
        nc, c = self.nc, self.c
        sbt = lambda name, shape, dt: es.enter_context(nc.sbuf_tensor(self.uniq(name), list(shape), dt)).ap()
        xb_aps = [sbt(f"nxb{i}", [128, NJ, bw], F32) for i in range(nbuf)]
        xb_ts = [[T(f"nxb{i}_{j}") for j in range(NJ)] for i in range(nbuf)]
        sq = Ring([sbt("nsq0", [128, NJ, bw], BF16)], "nsq")
        rs = Ring([sbt(f"nrs{i}", [128, bw], F32) for i in range(2)], "nrs")
        psr = self.psring([5, 6])
        XTv = self.XT.rearrange("j p t -> p j t")
        stash = {}
        blocks = []
        for bi, (t0, n, v) in enumerate(TBS if do_ctx else TBS[:4]):
            for off in range(0, n, bw):
                blocks.append((t0 + off, min(bw, n - off), v, bi))

        def stage1(k):
            t0, n, v, bi = blocks[k]
            xa, xts = xb_aps[k % nbuf], xb_ts[k % nbuf]
            c.dma(c.sp, xa[:, :, :n], XTv[:, :, t0:t0 + n], writes=xts)
            sa, st = sq.next()
            c.op(c.act, lambda: nc.scalar.activation(out=sa[:, :, :n], in_=xa[:, :, :n], func=AF.Square), reads=xts, writes=[st])
            pa, pt = psr.next()
            for j in range(NJ):
                self.mm(pa[:, :n], self.ones_bf, sa[:, j, :n], j == 0, j == NJ - 1, [st, self.Tconst], pt)
            stash[k] = (xa, xts, pa, pt)

        def stage2(k):
            t0, n, v, bi = blocks[k]
            xa, xts, pa, pt = stash.pop(k)
            ra, rt = rs.next()
            self.rstd_from_ps(pa[:, :n], pt, ra[:, :n], rt, 1.0 / D)
            for j in range(NJ):
                c.op(c.dve, lambda: nc.vector.scalar_tensor_tensor(out=xa[:, j, :n], in0=xa[:, j, :n], scalar=A[:, j, v:v + 1], in1=ra[:, :n],
                                                                   op0=ALU.mult, op1=ALU.mult), reads=[xts[j], rt, self.Tmod], writes=[xts[j]])
                c.op(c.act, lambda: nc.scalar.activation(out=h[:, j, t0:t0 + n], in_=xa[:, j, :n], func=AF.Identity,
                                                         bias=self.mod[:, Boff + j, v:v + 1], scale=1.0),
                     reads=[xts[j], self.Tmod], writes=[hts[bi][j]])

        stage1(0)
        for k in range(len(blocks)):
            if k + 1 < len(blocks):
                stage1(k + 1)
            stage2(k)

    def phase_win(self, l):
        nc, c = self.nc, self.c
        with ExitStack() as es:
            sbt = lambda name, shape, dt: es.enter_context(nc.sbuf_tensor(self.uniq(name), list(shape), dt)).ap()
            h = sbt("h", [128, NJ, NT], BF16)
            hts = [[T(f"h{i}_{j}") for j in range(NJ)] for i in range(5)]
            with ExitStack() as es2:
                self.norm_to_h(es2, self.A1, 0, h, hts)
                c.barrier()
            wr = Ring([sbt(f"ww{i}", [128, NJ, 512], BF16) for i in range(3)], "ww")
            stg = Ring([sbt(f"stg{i}", [128, NT], BF16) for i in range(3)], "stg")
            qf = Ring([sbt(f"qf{i}", [128, 512], F32) for i in range(4)], "qf")
            sqq = Ring([sbt(f"sqq{i}", [128, 512], BF16) for i in range(4)], "sqq")
            rsq = Ring([sbt(f"rsq{i}", [128, 512], F32) for i in range(4)], "rsq")
            qn = Ring([sbt(f"qn{i}", [128, 512], BF16) for i in range(4)], "qn")
            t1r = Ring([sbt(f"t1r{i}", [128, 512], F32) for i in range(2)], "t1r")
            t2r = Ring([sbt(f"t2r{i}", [128, 512], F32) for i in range(2)], "t2r")
            vtr = Ring([sbt(f"vtr{i}", [128, 256], BF16) for i in range(3)], "vtr")
            rope = sbt("rope", [128, 2, TL], BF16)
            Trope = T("rope")
            c.dma(c.sp, rope, self.k_rope.rearrange("a p t -> p a t"), writes=[Trope])
            psm = self.psring([0, 1, 2, 3])
            ps2 = self.psring([4, 5])
            ps3 = self.psring([6, 7])
            rotT = self.ksm[:, 128:256]
            cb = self.colsb
            wv = self.w_in[l].rearrange("(kc p) f -> p kc f", p=128)
            wgs = []
            for grp in range(3):
                wa, wt = wr.next()
                c.dma(c.pool, wa, wv[:, :, grp * 512:(grp + 1) * 512], writes=[wt])
                wgs.append((wa, wt))
            items = [(f, bi) for f in range(10) for bi in range(5)]
            stA, stB, fstage, fdone = {}, {}, {}, {}

            def finish_block(f):
                fdone[f] = fdone.get(f, 0) + 1
                if fdone[f] == 5:
                    sa, st = fstage.pop(f)
                    c.dma(c.sp, self.QK[f], sa, reads=[st])

            def stageA(i):
                f, bi = items[i]
                t0, n, v = TBS[bi]
                wa, wt = wgs[f // 4]
                ft = f % 4
                if f not in fstage:
                    fstage[f] = stg.next()
                pa, pt = psm.next()
                for kc in range(NJ):
                    self.mm(pa[:, :n], wa[:, kc, ft * 128:(ft + 1) * 128], h[:, kc, t0:t0 + n], kc == 0, kc == NJ - 1, [hts[bi][kc], wt], pt)
                qa, qt = qf.next()
                c.op(c.act, lambda: nc.scalar.copy(out=qa[:, :n], in_=pa[:, :n]), reads=[pt], writes=[qt])
                s2, s2t = sqq.next()
                c.op(c.act, lambda: nc.scalar.activation(out=s2[:, :n], in_=pa[:, :n], func=AF.Square), reads=[pt], writes=[s2t])
                stA[i] = (qa, qt, s2, s2t)

            def stageB(i):
                f, bi = items[i]
                t0, n, v = TBS[bi]
                qa, qt, s2, s2t = stA.pop(i)
                sa, st = fstage[f]
                gcol = cb[:, C_QG:C_QG + 1] if f < 8 else cb[:, C_KG:C_KG + 1]
                p2, p2t = ps2.next()
                self.mm(p2[:, :n], self.ones_bf, s2[:, :n], True, True, [s2t, self.Tconst], p2t)
                ra, rt = rsq.next()
                self.rstd_from_ps(p2[:, :n], p2t, ra[:, :n], rt, 1.0 / 128)
                if v == 1:
                    c.op(c.dve, lambda: nc.vector.scalar_tensor_tensor(out=sa[:, t0:t0 + n], in0=qa[:, :n], scalar=gcol, in1=ra[:, :n],
                                                                       op0=ALU.mult, op1=ALU.mult), reads=[qt, rt, self.Tcols], writes=[st])
                    finish_block(f)
                else:
                    na, nt_ = qn.next()
                    c.op(c.dve, lambda: nc.vector.scalar_tensor_tensor(out=na[:, :n], in0=qa[:, :n], scalar=gcol, in1=ra[:, :n],
                                                                       op0=ALU.mult, op1=ALU.mult), reads=[qt, rt, self.Tcols], writes=[nt_])
                    stB[i] = (na, nt_)

            def stageC(i):
                f, bi = items[i]
                t0, n, v = TBS[bi]
                if v == 1:
                    return
                na, nt_ = stB.pop(i)
                sa, st = fstage[f]
                p3, p3t = ps3.next()
                self.mm(p3[:, :n], rotT, na[:, :n], True, True, [nt_, self.Tconst], p3t)
                a1, a1t = t1r.next()
                c.op(c.pool, lambda: nc.gpsimd.tensor_tensor(out=a1[:, :n], in0=na[:, :n], in1=rope[:, 0, t0:t0 + n], op=ALU.mult),
                     reads=[nt_, Trope], writes=[a1t])
                a2, a2t = t2r.next()
                c.op(c.dve, lambda: nc.vector.tensor_tensor(out=a2[:, :n], in0=p3[:, :n], in1=rope[:, 1, t0:t0 + n], op=ALU.mult),
                     reads=[p3t, Trope], writes=[a2t])
                c.op(c.pool, lambda: nc.gpsimd.tensor_tensor(out=sa[:, t0:t0 + n], in0=a1[:, :n], in1=a2[:, :n], op=ALU.add),
                     reads=[a1t, a2t], writes=[st])
                finish_block(f)

            NI = len(items)
            for i in range(NI + 2):
                if i < NI:
                    stageA(i)
                if 0 <= i - 1 < NI:
                    stageB(i - 1)
                if 0 <= i - 2 < NI:
                    stageC(i - 2)
            wa, wt = wgs[2]
            for tt in range(18):
                pa, pt = psm.next()
                for kc in range(NJ):
                    self.mm(pa[:, 0:256], h[:, kc, tt * 128:(tt + 1) * 128], wa[:, kc, 256:512], kc == 0, kc == NJ - 1,
                            [hts[min(tt // 4, 4)][kc], wt], pt)
                va, vt = vtr.next()
                c.op(c.act, lambda: nc.scalar.copy(out=va, in_=pa[:, 0:256]), reads=[pt], writes=[vt])
                c.dma(c.sp, self.VT[tt], va, reads=[vt])
            ke = 0
            for grp in range(3, 9):
                wa, wt = wr.next()
                c.dma(c.pool, wa, wv[:, :, grp * 512:(grp + 1) * 512], writes=[wt])
                for ft in range(4):
                    f = grp * 4 + ft
                    sa, st = stg.next()
                    for bi, (t0, n, v) in enumerate(TBS):
                        pa, pt = psm.next()
                        for kc in range(NJ):
                            self.mm(pa[:, :n], wa[:, kc, ft * 128:(ft + 1) * 128], h[:, kc, t0:t0 + n], kc == 0, kc == NJ - 1, [hts[bi][kc], wt], pt)
                        if ke % 2 == 0:
                            c.op(c.act, lambda: nc.scalar.copy(out=sa[:, t0:t0 + n], in_=pa[:, :n]), reads=[pt], writes=[st])
                        else:
                            c.op(c.dve, lambda: nc.vector.tensor_copy(out=sa[:, t0:t0 + n], in_=pa[:, :n]), reads=[pt], writes=[st])
                        ke += 1
                    c.dma(c.sp, self.U[f - 12], sa, reads=[st])
            wg = self.w_gate[l].rearrange("(kc p) f -> p kc f", p=128)
            for grp in range(12):
                wa, wt = wr.next()
                c.dma(c.pool, wa, wg[:, :, grp * 512:(grp + 1) * 512], writes=[wt])
                for ft in range(4):
                    f = grp * 4 + ft
                    sa, st = stg.next()
                    for bi, (t0, n, v) in enumerate(TBS):
                        pa, pt = psm.next()
                        for kc in range(NJ):
                            self.mm(pa[:, :n], wa[:, kc, ft * 128:(ft + 1) * 128], h[:, kc, t0:t0 + n], kc == 0, kc == NJ - 1, [hts[bi][kc], wt], pt)
                        c.op(c.act, lambda: nc.scalar.activation(out=sa[:, t0:t0 + n], in_=pa[:, :n], func=AF.Sigmoid,
                                                                 bias=cb[:, C_BG + f:C_BG + f + 1], scale=1.0), reads=[pt, self.Tcols], writes=[st])
                    c.dma(c.sp, self.G[f], sa, reads=[st])
            c.barrier()

    def phase_attn(self, l, do_ctx=True):
        nc, c = self.nc, self.c
        with ExitStack() as es:
            sbt = lambda name, shape, dt: es.enter_context(nc.sbuf_tensor(self.uniq(name), list(shape), dt)).ap()
            q = sbt("q", [128, 8, NT], BF16)
            k = sbt("k", [128, 2, NT], BF16)
            vt = sbt("vt", [128, 18, 256], BF16)
            Tq, Tk, Tv = T("q"), T("k"), T("v")
            QKv = self.QK.rearrange("f p t -> p f t")
            c.dma(c.sp, k, QKv[:, 8:10, :], writes=[Tk])
            c.dma(c.sp, vt, self.VT.rearrange("a p e -> p a e"), writes=[Tv])
            c.dma(c.sp, q, QKv[:, 0:8, :], writes=[Tq])
            ptr = Ring([sbt(f"pt{i}", [128, 512], BF16) for i in range(4)], "pt")
            rl = Ring([sbt(f"rl{i}", [128, 512], F32) for i in range(2)], "rl")
            stg = Ring([sbt(f"ostg{i}", [128, 512], BF16) for i in range(3)], "ostg")
            pss = self.psring([0, 1, 2, 3])
            pso = self.psring([4, 5])
            psl = self.psring([6, 7])
            scale = 1.0 / np.sqrt(128.0)

            def attend(hd, q0, nq, kts):
                g = hd // 4
                oa, ot = pso.next()
                la, lt = psl.next()
                nk = len(kts)
                S = []

                def emit_s(i):
                    kt = kts[i]
                    sa, st = pss.next()
                    self.mm(sa[:, :nq], k[:, g, kt * 128:(kt + 1) * 128], q[:, hd, q0:q0 + nq], True, True, [Tk, Tq], st)
                    S.append((sa, st))

                emit_s(0)
                if nk > 1:
                    emit_s(1)
                for i, kt in enumerate(kts):
                    sa, st = S[i]
                    pa, pt_ = ptr.next()
                    c.op(c.act, lambda: nc.scalar.activation(out=pa[:, :nq], in_=sa[:, :nq], func=AF.Exp, scale=scale), reads=[st], writes=[pt_])
                    if i + 2 < nk:
                        emit_s(i + 2)
                    self.mm(oa[:, :nq], vt[:, kt, g * 128:(g + 1) * 128], pa[:, :nq], i == 0, i == nk - 1, [Tv, pt_], ot)
                    self.mm(la[:, :nq], self.ones_bf, pa[:, :nq], i == 0, i == nk - 1, [self.Tconst, pt_], lt, sig=True)
                ra, rt = rl.next()
                c.op(c.dve, lambda: nc.vector.reciprocal(out=ra[:, :nq], in_=la[:, :nq]), reads=[lt], writes=[rt])
                ga, gt = stg.next()
                c.op(c.dve, lambda: nc.vector.tensor_tensor(out=ga[:, :nq], in0=oa[:, :nq], in1=ra[:, :nq], op=ALU.mult), reads=[ot, rt], writes=[gt])
                c.dma(c.sp, self.YA[hd][:, q0:q0 + nq], ga[:, :nq], reads=[gt])

            for hd in range(8):
                for qb in range(4):
                    attend(hd, qb * 512, 512, list(range(18)))
                if do_ctx:
                    attend(hd, TL, TC, [16, 17])
            c.barrier()

    def phase_lru(self, l):
        nc, c = self.nc, self.c
        with ExitStack() as es:
            sbt = lambda name, shape, dt: es.enter_context(nc.sbuf_tensor(self.uniq(name), list(shape), dt)).ap()
            cb = self.colsb
            bd = sbt("bd", [128, 2, 2, 8, 128], BF16)
            Tbd = T("bd")
            c.op(c.pool, lambda: nc.gpsimd.memset(bd, 0.0), writes=[Tbd])
            for d in range(2):
                for ri, w in enumerate((self.lru_w_r, self.lru_w_i)):
                    wv = w[l, d].rearrange("(j two) c e -> two c j e", two=2)
                    c.dma(c.pool, bd[0:64, d, ri, :, 0:64], wv[0], writes=[Tbd])
                    c.dma(c.pool, bd[64:128, d, ri, :, 64:128], wv[1], writes=[Tbd])
            mk = lambda name, dt, k=1, w=NT: Ring([sbt(f"{name}{i}", [128, w], dt) for i in range(k)], name)
            ur = mk("lu", BF16, 3)
            ugr = mk("lug", BF16, 3)
            vr = mk("lv", F32, 2)
            vbr = mk("lvb", BF16, 2)
            rr = mk("lr", F32, 2)
            ir = mk("li", F32, 2)
            ar = mk("la", F32, 2)
            er = mk("le", F32, 2)
            hr = mk("lh", F32, 2)
            yr = mk("ly", F32, 2)
            gr = mk("lg", F32, 1)
            sr = mk("lst", BF16, 2)
            psr = self.psring([0, 1, 2, 3, 4, 5, 6, 7])
            segs = [(0, TL), (TL, TC)]

            def rev(ap, t0, n):
                a = ap[:, t0:t0 + n]
                pstep = a.ap[0][0]
                return bass.AP(a.tensor, a.offset + (n - 1), [[pstep, 128], [-1, n]])

            def load(j):
                ua, ut = ur.next()
                c.dma(c.sp, ua, self.U[j], writes=[ut])
                uga, ugt = ugr.next()
                c.dma(c.sp, uga, self.U[8 + j], writes=[ugt])
                return ua, ut, uga, ugt

            def prep(j, ld):
                ua, ut, uga, ugt = ld
                va, vt_ = vr.next()
                wcol = lambda tap: cb[:, C_CW + tap * 8 + j:C_CW + tap * 8 + j + 1]
                c.op(c.dve, lambda: nc.vector.tensor_scalar(out=va, in0=ua, scalar1=wcol(2), scalar2=cb[:, C_CB + j:C_CB + j + 1],
                                                            op0=ALU.mult, op1=ALU.add), reads=[ut, self.Tcols], writes=[vt_])
                for (s0, sn) in segs:
                    for tap, sh in ((0, -2), (1, -1), (3, 1)):
                        lo = max(0, -sh)
                        hi = sn - max(0, sh)
                        c.op(c.dve, lambda: nc.vector.scalar_tensor_tensor(out=va[:, s0 + lo:s0 + hi], in0=ua[:, s0 + lo + sh:s0 + hi + sh], scalar=wcol(tap),
                                                                           in1=va[:, s0 + lo:s0 + hi], op0=ALU.mult, op1=ALU.add),
                             reads=[ut, vt_, self.Tcols], writes=[vt_])
                vba, vbt = vbr.next()
                c.op(c.pool, lambda: nc.gpsimd.tensor_copy(out=vba, in_=va), reads=[vt_], writes=[vbt])
                return va, vt_, vba, vbt, uga, ugt

            def stageS(j, d, pr):
                va, vt_, vba, vbt, uga, ugt = pr
                col = d * 8 + j
                ra, rt = rr.next()
                ia, it = ir.next()
                for (t0, n, v) in TBS:
                    pa, pt = psr.next()
                    self.mm(pa[:, :n], bd[:, d, 0, j, :], vba[:, t0:t0 + n], True, True, [Tbd, vbt], pt)
                    c.op(c.act, lambda: nc.scalar.activation(out=ra[:, t0:t0 + n], in_=pa[:, :n], func=AF.Sigmoid,
                                                             bias=cb[:, C_BR + col:C_BR + col + 1], scale=1.0), reads=[pt, self.Tcols], writes=[rt])
                    pa, pt = psr.next()
                    self.mm(pa[:, :n], bd[:, d, 1, j, :], vba[:, t0:t0 + n], True, True, [Tbd, vbt], pt)
                    c.op(c.act, lambda: nc.scalar.activation(out=ia[:, t0:t0 + n], in_=pa[:, :n], func=AF.Sigmoid,
                                                             bias=cb[:, C_BI + col:C_BI + col + 1], scale=1.0), reads=[pt, self.Tcols], writes=[it])
                return ra, rt, ia, it

            def stageE(j, d, pr, sres):
                va, vt_, vba, vbt, uga, ugt = pr
                ra, rt, ia, it = sres
                col = d * 8 + j
                aa, at = ar.next()
                ea, et = er.next()
                c.op(c.act, lambda: nc.scalar.activation(out=aa, in_=ra, func=AF.Exp, scale=self.nsp[:, col:col + 1]), reads=[rt, self.Tmod], writes=[at])
                c.op(c.pool, lambda: nc.gpsimd.tensor_tensor(out=ia, in0=ia, in1=va, op=ALU.mult), reads=[it, vt_], writes=[it])
                c.op(c.pool, lambda: nc.gpsimd.tensor_tensor(out=ea, in0=aa, in1=aa, op=ALU.mult), reads=[at], writes=[et])
                return aa, at, ea, et, ia, it

            def stageY(j, d, eres, ya, yt):
                aa, at, ea, et, ia, it = eres
                c.op(c.act, lambda: nc.scalar.activation(out=ea, in_=ea, func=AF.Sqrt, bias=1.0, scale=-1.0), reads=[et], writes=[et])
                c.op(c.dve, lambda: nc.vector.tensor_tensor(out=ia, in0=ia, in1=ea, op=ALU.mult), reads=[it, et], writes=[it])
                if d == 0:
                    ha, ht = ya, yt
                    c.op(c.dve, lambda: nc.vector.tensor_tensor_scan(out=ha[:, TL:NT], data0=aa[:, TL:NT], data1=ia[:, TL:NT], initial=0.0,
                                                                     op0=ALU.mult, op1=ALU.add), reads=[at, it], writes=[ht])
                    c.op(c.dve, lambda: nc.vector.tensor_tensor_scan(out=ha[:, 0:TL], data0=aa[:, 0:TL], data1=ia[:, 0:TL], initial=ha[:, NT - 1:NT],
                                                                     op0=ALU.mult, op1=ALU.add), reads=[at, it, ht], writes=[ht])
                else:
                    ha, ht = hr.next()
                    c.op(c.dve, lambda: nc.vector.tensor_tensor_scan(out=rev(ha, TL, TC), data0=rev(aa, TL, TC), data1=rev(ia, TL, TC), initial=0.0,
                                                                     op0=ALU.mult, op1=ALU.add), reads=[at, it], writes=[ht])
                    c.op(c.dve, lambda: nc.vector.tensor_tensor_scan(out=rev(ha, 0, TL), data0=rev(aa, 0, TL), data1=rev(ia, 0, TL), initial=ha[:, TL:TL + 1],
                                                                     op0=ALU.mult, op1=ALU.add), reads=[at, it, ht], writes=[ht])
                    c.op(c.pool, lambda: nc.gpsimd.tensor_tensor(out=ya, in0=ya, in1=ha, op=ALU.add), reads=[yt, ht], writes=[yt])

            def tail(j, pr, ya, yt):
                va, vt_, vba, vbt, uga, ugt = pr
                ga, gt = gr.next()
                c.op(c.act, lambda: nc.scalar.activation(out=ga, in_=uga, func=AF.Gelu_apprx_tanh), reads=[ugt], writes=[gt])
                sa, st = sr.next()
                c.op(c.dve, lambda: nc.vector.tensor_tensor(out=sa, in0=ya, in1=ga, op=ALU.mult), reads=[yt, gt], writes=[st])
                c.dma(c.sp, self.YR[j], sa, reads=[st])

            lds = {0: load(0), 1: load(1)}
            pr = prep(0, lds.pop(0))
            for j in range(8):
                if j + 2 < 8:
                    lds[j + 2] = load(j + 2)
                ya, yt = yr.next()
                s0 = stageS(j, 0, pr)
                s1 = stageS(j, 1, pr)
                e0 = stageE(j, 0, pr, s0)
                e1 = stageE(j, 1, pr, s1)
                stageY(j, 0, e0, ya, yt)
                pr_next = prep(j + 1, lds.pop(j + 1)) if j + 1 < 8 else None
                stageY(j, 1, e1, ya, yt)
                tail(j, pr, ya, yt)
                pr = pr_next
            c.barrier()

    def phase_fourier(self, l, do_ctx=True):
        nc, c = self.nc, self.c
        with ExitStack() as es:
            sbt = lambda name, shape, dt: es.enter_context(nc.sbuf_tensor(self.uniq(name), list(shape), dt)).ap()
            A = sbt("fA", [128, 8, 18, 256], BF16)
            At = [T(f"fA{g}") for g in range(8)]
            ufr = Ring([sbt(f"fu{i}", [128, NT], BF16) for i in range(2)], "fu")
            tabr = Ring([sbt(f"ftab{i}", [128, 2, 16, 512], BF16) for i in range(2)], "ftab")
            tabc = sbt("ftabc", [128, 2, 2, 256], BF16)
            Ttc = T("ftabc")
            stg = Ring([sbt(f"fstg{i}", [128, 512], BF16) for i in range(3)], "fstg")
            c.dma(c.sp, tabc, self.k_dftc.rearrange("a (tt p) u -> p a tt u", p=128), writes=[Ttc])
            csG = self.ksm[:, 256:512]
            ps1 = self.psring([0, 1, 2, 3])
            ps2 = self.psring([4, 5, 6, 7])
            ke = 0
            def preu(g):
                ua, ut = ufr.next()
                c.dma(c.sp, ua, self.U[16 + g], writes=[ut])
                return ua, ut

            nxtu = preu(0)
            for g in range(8):
                ua, ut = nxtu
                if g + 1 < 8:
                    nxtu = preu(g + 1)
                for tt in range(18):
                    pa, pt = ps1.next()
                    self.mm(pa[:, 0:256], ua[:, tt * 128:(tt + 1) * 128], csG, True, True, [ut, self.Tconst], pt)
                    if ke % 2 == 0:
                        c.op(c.act, lambda: nc.scalar.copy(out=A[:, g, tt, :], in_=pa[:, 0:256]), reads=[pt], writes=[At[g]])
                    else:
                        c.op(c.dve, lambda: nc.vector.tensor_copy(out=A[:, g, tt, :], in_=pa[:, 0:256]), reads=[pt], writes=[At[g]])
                    ke += 1
            dv = self.k_dft.rearrange("a (tt p) u -> p a tt u", p=128)
            def pret(tb):
                ta, tt_ = tabr.next()
                c.dma(c.sp, ta[:, 0], dv[:, 0, :, tb * 512:(tb + 1) * 512], writes=[tt_])
                c.dma(c.sp, ta[:, 1], dv[:, 1, :, tb * 512:(tb + 1) * 512], writes=[tt_])
                return ta, tt_

            nxtt = pret(0)
            for tb in range(4):
                ta, tt_ = nxtt
                if tb + 1 < 4:
                    nxtt = pret(tb + 1)
                for g in range(8):
                    pa, pt = ps2.next()
                    for tt in range(16):
                        self.mm(pa, A[:, g, tt, 0:128], ta[:, 0, tt, :], tt == 0, False, [At[g], tt_], pt, sig=False)
                        self.mm(pa, A[:, g, tt, 128:256], ta[:, 1, tt, :], False, tt == 15, [At[g], tt_], pt)
                    sa, st = stg.next()
                    c.op(c.act, lambda: nc.scalar.mul(out=sa, in_=pa, mul=1.0 / 512.0), reads=[pt], writes=[st])
                    c.dma(c.sp, self.YF[g][:, tb * 512:(tb + 1) * 512], sa, reads=[st])
            sc_c = 1.0 / np.sqrt(TC * 128.0)
            for g in (range(8) if do_ctx else ()):
                pa, pt = ps2.next()
                for tt in range(2):
                    self.mm(pa[:, 0:256], A[:, g, 16 + tt, 0:128], tabc[:, 0, tt, :], tt == 0, False, [At[g], Ttc], pt, sig=False)
                    self.mm(pa[:, 0:256], A[:, g, 16 + tt, 128:256], tabc[:, 1, tt, :], False, tt == 1, [At[g], Ttc], pt)
                sa, st = stg.next()
                c.op(c.act, lambda: nc.scalar.mul(out=sa[:, 0:256], in_=pa[:, 0:256], mul=float(sc_c)), reads=[pt], writes=[st])
                c.dma(c.sp, self.YF[g][:, TL:NT], sa[:, 0:256], reads=[st])
            c.barrier()

    def phase_merge(self, l, do_ctx=True):
        nc, c = self.nc, self.c
        with ExitStack() as es:
            sbt = lambda name, shape, dt: es.enter_context(nc.sbuf_tensor(self.uniq(name), list(shape), dt)).ap()
            ys = [sbt(f"my{k}", [128, 8, NT], BF16) for k in range(3)]
            Ty = [T(f"my{k}") for k in range(3)]
            for k, src in enumerate((self.YA, self.YR, self.YF)):
                c.dma(c.sp, ys[k], src.rearrange("f p t -> p f t"), writes=[Ty[k]])
            wr = Ring([sbt(f"mw{i}", [128, 3, 8, 128], BF16) for i in range(3)], "mw")
            gr = Ring([sbt(f"mg{i}", [128, 3, NT], BF16) for i in range(2)], "mg")
            stg = Ring([sbt(f"mstg{i}", [128, NT], BF16) for i in range(2)], "mstg")
            tr = [Ring([sbt(f"mt{k}_{i}", [128, 512], F32) for i in range(2)], f"mt{k}") for k in range(3)]
            psr = self.psring([0, 1, 2, 3, 4, 5, 6, 7])
            wbv = self.w_branch[l]
            Gv = self.G.rearrange("(k dt) p t -> dt p k t", k=3)
            def pre(dt_):
                wa, wt = wr.next()
                c.dma(c.pool, wa, wbv[dt_], writes=[wt])
                ga, gt = gr.next()
                c.dma(c.sp, ga, Gv[dt_], writes=[gt])
                return wa, wt, ga, gt

            nxt = pre(0)
            for dt_ in range(NJ):
                wa, wt, ga, gt = nxt
                if dt_ + 1 < NJ:
                    nxt = pre(dt_ + 1)
                sa, st = stg.next()
                for (t0, n, v) in (TBS if do_ctx else TBS[:4]):
                    tk = []
                    for k in range(3):
                        pa, pt = psr.next()
                        for kc in range(8):
                            self.mm(pa[:, :n], wa[:, k, kc, :], ys[k][:, kc, t0:t0 + n], kc == 0, kc == 7, [wt, Ty[k]], pt)
                        ta, tt_ = tr[k].next()
                        c.op(c.dve, lambda: nc.vector.tensor_tensor(out=ta[:, :n], in0=pa[:, :n], in1=ga[:, k, t0:t0 + n], op=ALU.mult),
                             reads=[pt, gt], writes=[tt_])
                        tk.append((ta, tt_))
                    c.op(c.pool, lambda: nc.gpsimd.tensor_tensor(out=tk[0][0][:, :n], in0=tk[0][0][:, :n], in1=tk[1][0][:, :n], op=ALU.add),
                         reads=[tk[0][1], tk[1][1]], writes=[tk[0][1]])
                    c.op(c.pool, lambda: nc.gpsimd.tensor_tensor(out=sa[:, t0:t0 + n], in0=tk[0][0][:, :n], in1=tk[2][0][:, :n], op=ALU.add),
                         reads=[tk[0][1], tk[2][1]], writes=[st])
                c.dma(c.sp, self.MRG[dt_], sa, reads=[st])
            c.barrier()

    def proj_residual(self, src, nk, wview, gate_off, halves, side_mod=None):
        nc, c = self.nc, self.c
        for hi, blocks in enumerate(halves):
            h0 = blocks[0][0]
            hn = blocks[-1][0] + blocks[-1][1] - h0
            with ExitStack() as es:
                sbt = lambda name, shape, dt: es.enter_context(nc.sbuf_tensor(self.uniq(name), list(shape), dt)).ap()
                a = sbt("pa_act", [128, nk, hn], BF16)
                Ta = T("pa_act")
                srcv = src.rearrange("f p t -> p f t")
                half_k = nk // 2
                c.dma(c.sp, a[:, :half_k, :], srcv[:, :half_k, h0:h0 + hn], writes=[Ta])
                c.dma(c.sp, a[:, half_k:, :], srcv[:, half_k:, h0:h0 + hn], writes=[Ta])
                wr = Ring([sbt(f"pw{i}", [128, nk, 128], BF16) for i in range(3)], "pw")
                xr = Ring([sbt(f"px{i}", [128, hn], F32) for i in range(3)], "px")
                psr = self.psring([0, 1, 2, 3, 4, 5, 6])
                if side_mod is not None:
                    emit_load, emit_mm = self.mod_emitter(side_mod, es)

                def pre(dt_):
                    wa, wt = wr.next()
                    c.dma(c.pool, wa, wview[dt_], writes=[wt])
                    xa, xt = xr.next()
                    c.dma(c.sp, xa, self.XT[dt_][:, h0:h0 + hn], writes=[xt])
                    return wa, wt, xa, xt

                nxt = pre(0)
                for dt_ in range(NJ):
                    wa, wt, xa, xt = nxt
                    if dt_ + 1 < NJ:
                        nxt = pre(dt_ + 1)
                    if side_mod is not None:
                        it = hi * NJ + dt_
                        g_lo, g_hi = (it * 3) // 2, ((it + 1) * 3) // 2
                        if dt_ == 0:
                            pend = []
                        for g in pend:
                            emit_mm(g)
                        for g in range(g_lo, g_hi):
                            emit_load(g)
                        pend = list(range(g_lo, g_hi))
                        if dt_ == NJ - 1:
                            for g in pend:
                                emit_mm(g)
                            pend = []
                    for (t0, n, v) in blocks:
                        pa, pt = psr.next()
                        for kc in range(nk):
                            self.mm(pa[:, :n], wa[:, kc, :], a[:, kc, t0 - h0:t0 - h0 + n], kc == 0, kc == nk - 1, [wt, Ta], pt)
                        c.op(c.dve, lambda: nc.vector.scalar_tensor_tensor(out=xa[:, t0 - h0:t0 - h0 + n], in0=pa[:, :n],
                                                                           scalar=self.mod[:, gate_off + dt_, v:v + 1],
                                                                           in1=xa[:, t0 - h0:t0 - h0 + n], op0=ALU.mult, op1=ALU.add),
                             reads=[pt, xt, self.Tmod], writes=[xt])
                    c.dma(c.sp, self.XT[dt_][:, h0:h0 + hn], xa, reads=[xt])
                c.barrier()

    def phase_ffn_in(self, l, side_mod=None, do_ctx=True):
        nc, c = self.nc, self.c
        with ExitStack() as es:
            sbt = lambda name, shape, dt: es.enter_context(nc.sbuf_tensor(self.uniq(name), list(shape), dt)).ap()
            h = sbt("h2", [128, NJ, NT], BF16)
            hts = [[T(f"h2{i}_{j}") for j in range(NJ)] for i in range(5)]
            wr = Ring([sbt(f"fw{i}", [128, NJ, 512], BF16) for i in range(3)], "fw")
            stg = Ring([sbt(f"fs{i}", [128, NT], BF16) for i in range(2)], "fs")
            sg = Ring([sbt(f"fsg{i}", [128, 512], F32) for i in range(2)], "fsg")
            GW = 128
            NG = 12288 // GW
            if side_mod is not None:
                emit_load, emit_mm = self.mod_emitter(side_mod, es, gw=GW)
            with ExitStack() as es2:
                self.norm_to_h(es2, self.A2, 48, h, hts, bw=256, nbuf=2, do_ctx=do_ctx)
            psg = self.psring([0, 1, 2, 3])
            psu = self.psring([4, 5, 6])
            wv = self.w_ffn_in[l].rearrange("(kc p) f -> p kc f", p=128)
            pend = []
            for grp in range(11):
                wga, wgt = wr.next()
                c.dma(c.pool, wga, wv[:, :, grp * 512:(grp + 1) * 512], writes=[wgt])
                wua, wut = wr.next()
                c.dma(c.pool, wua, wv[:, :, DFF + grp * 512:DFF + (grp + 1) * 512], writes=[wut])
                for ft in range(4):
                    f = grp * 4 + ft
                    sa, st = stg.next()
                    for bi, (t0, n, v) in enumerate(TBS if do_ctx else TBS[:4]):
                        if side_mod is not None:
                            tick = f * 5 + bi
                            if tick % 2 == 0 and tick // 2 < NG:
                                for g in pend:
                                    emit_mm(g)
                                emit_load(tick // 2)
                                pend = [tick // 2]
                        pg, pgt = psg.next()
                        for kc in range(NJ):
                            self.mm(pg[:, :n], wga[:, kc, ft * 128:(ft + 1) * 128], h[:, kc, t0:t0 + n], kc == 0, kc == NJ - 1, [hts[bi][kc], wgt], pgt)
                        pu, put = psu.next()
                        for kc in range(NJ):
                            self.mm(pu[:, :n], wua[:, kc, ft * 128:(ft + 1) * 128], h[:, kc, t0:t0 + n], kc == 0, kc == NJ - 1, [hts[bi][kc], wut], put)
                        ga, gt = sg.next()
                        c.op(c.act, lambda: nc.scalar.activation(out=ga[:, :n], in_=pg[:, :n], func=AF.Silu), reads=[pgt], writes=[gt])
                        c.op(c.dve, lambda: nc.vector.tensor_tensor(out=sa[:, t0:t0 + n], in0=pu[:, :n], in1=ga[:, :n], op=ALU.mult),
                             reads=[put, gt], writes=[st])
                    c.dma(c.sp, self.HID[f], sa, reads=[st])
            for g in pend:
                emit_mm(g)
            c.barrier()

    def phase_final(self):
        nc, c = self.nc, self.c
        with ExitStack() as es:
            sbt = lambda name, shape, dt: es.enter_context(nc.sbuf_tensor(self.uniq(name), list(shape), dt)).ap()
            xb = Ring([sbt(f"zxb{i}", [128, NJ, 512], F32) for i in range(2)], "zxb")
            sq = Ring([sbt("zsq0", [128, NJ, 512], F32)], "zsq")
            rs = Ring([sbt(f"zrs{i}", [128, 512], F32) for i in range(2)], "zrs")
            orow = Ring([sbt(f"zo{i}", [128, D], F32) for i in range(2)], "zo")
            psr = self.psring([6, 7])
            pst = self.psring([0, 1, 2, 3])
            XTv = self.XT.rearrange("j p t -> p j t")
            cb = self.colsb
            k = 0
            for (t0, n, v) in TBS[:4]:
                xa, xt = xb.next()
                c.dma(c.sp, xa, XTv[:, :, t0:t0 + n], writes=[xt])
                sa, st = sq.next()
                c.op(c.act, lambda: nc.scalar.activation(out=sa, in_=xa, func=AF.Square), reads=[xt], writes=[st])
                pa, pt = psr.next()
                for j in range(NJ):
                    self.mm(pa, self.ones32, sa[:, j, :], j == 0, j == NJ - 1, [st, self.Tconst], pt)
                ra, rt = rs.next()
                self.rstd_from_ps(pa, pt, ra, rt, 1.0 / D)
                for j in range(NJ):
                    c.op(c.dve, lambda: nc.vector.scalar_tensor_tensor(out=xa[:, j, :], in0=xa[:, j, :], scalar=cb[:, C_GF + j:C_GF + j + 1], in1=ra,
                                                                       op0=ALU.mult, op1=ALU.mult), reads=[xt, rt, self.Tcols], writes=[xt])
                for tt in range(4):
                    oa, ot = orow.next()
                    for j4 in range(4):
                        qa, qt = pst.next()
                        for jj in range(4):
                            j = j4 * 4 + jj
                            c.op(c.pe, lambda: nc.tensor.transpose(qa[:, jj * 128:(jj + 1) * 128], xa[:, j, tt * 128:(tt + 1) * 128], self.ident32),
                                 reads=[xt, self.Tconst], writes=[qt], sig=(jj == 3))
                        if k % 2 == 0:
                            c.op(c.act, lambda: nc.scalar.copy(out=oa[:, j4 * 512:(j4 + 1) * 512], in_=qa), reads=[qt], writes=[ot])
                        else:
                            c.op(c.dve, lambda: nc.vector.tensor_copy(out=oa[:, j4 * 512:(j4 + 1) * 512], in_=qa), reads=[qt], writes=[ot])
                        k += 1
                    r0 = t0 + tt * 128
                    c.dma(c.sp, self.out[r0:r0 + 128, :], oa, reads=[ot])
            c.barrier()

    def build(self, upto=None):
        self.phase_init()
        for l in range(self.n_layers):
            last = (l == self.n_layers - 1)
            dc = not last
            halves = HALF_BLOCKS if dc else HALF_BLOCKS_LAST
            self.phase_mod(l, standalone=(l == 0))
            self.phase_win(l)
            self.phase_attn(l, do_ctx=dc)
            self.phase_lru(l)
            self.phase_fourier(l, do_ctx=dc)
            self.phase_merge(l, do_ctx=dc)
            self.proj_residual(self.MRG, NJ, self.w_out[l], 32, halves)
            self.phase_ffn_in(l, side_mod=(l + 1 if not last else None), do_ctx=dc)
            self.proj_residual(self.HID, 44, self.w_ffn_out[l], 80, halves, side_mod=None)
        self.phase_final()
        return self.nc


def _col(v):
    v = np.asarray(v, np.float32)
    return np.ascontiguousarray(v.reshape(-1, 128).T)


def make_consts():
    bf = ml_dtypes.bfloat16
    t = np.arange(TL)
    row = (t // 64).astype(np.float64)
    colp = (t % 64).astype(np.float64)
    inv = 10000.0 ** (-np.arange(32, dtype=np.float64) / 32)
    ang = np.concatenate([row[:, None] * inv, colp[:, None] * inv], axis=-1)
    angT = np.concatenate([ang.T, ang.T], axis=0)
    k_rope = np.stack([np.cos(angT), np.sin(angT)]).astype(np.float32).astype(bf)
    ident = np.eye(128, dtype=np.float32)
    rotT = np.zeros((128, 128), np.float32)
    for m in range(64):
        rotT[m + 64, m] = -1.0
        rotT[m, m + 64] = 1.0
    cc = np.arange(128)
    ph = 2 * np.pi * ((cc[:, None] * cc[None, :]) % 128) / 128.0
    k_small = np.concatenate([ident, rotT, np.cos(ph), np.sin(ph)], axis=1).astype(np.float32).astype(bf)
    th = 2 * np.pi * ((t[:, None].astype(np.int64) * t[None, :]) % TL) / float(TL)
    k_dft = np.stack([np.cos(th), -np.sin(th)]).astype(np.float32).astype(bf)
    tc = np.arange(TC)
    thc = 2 * np.pi * ((tc[:, None] * tc[None, :]) % TC) / float(TC)
    k_dftc = np.stack([np.cos(thc), -np.sin(thc)]).astype(np.float32).astype(bf)
    return dict(k_rope=k_rope, k_small=k_small, k_ident32=ident, k_dft=k_dft, k_dftc=k_dftc)


def make_cols(inp):
    cols = np.zeros((DEPTH, 128, NCOL), np.float32)
    for l in range(DEPTH):
        cols[l, :, C_BMOD:C_BMOD + 96] = _col(inp["b_mod"][l])
        cols[l, :, C_GN1:C_GN1 + 16] = _col(inp["g_norm1"][l])
        cols[l, :, C_GN2:C_GN2 + 16] = _col(inp["g_norm2"][l])
        cols[l, :, C_BG:C_BG + 48] = _col(inp["b_gate"][l])
        for tap in range(4):
            cols[l, :, C_CW + tap * 8:C_CW + tap * 8 + 8] = _col(inp["conv_w"][l, tap])
        cols[l, :, C_CB:C_CB + 8] = _col(inp["conv_b"][l])
        for d in range(2):
            cols[l, :, C_BR + d * 8:C_BR + d * 8 + 8] = _col(inp["lru_b_r"][l, d])
            cols[l, :, C_BI + d * 8:C_BI + d * 8 + 8] = _col(inp["lru_b_i"][l, d])
            cols[l, :, C_LAM + d * 8:C_LAM + d * 8 + 8] = _col(inp["lru_lambda"][l, d])
        cols[l, :, C_QG] = inp["q_gain"][l]
        cols[l, :, C_KG] = inp["k_gain"][l]
        cols[l, :, C_GF:C_GF + 16] = _col(inp["g_final"])
    return cols


def make_in_maps(inp, cores):
    inp = {k: np.asarray(v) for k, v in inp.items()}
    consts = make_consts()
    cols = make_cols(inp)
    shared = dict(cols=cols, **consts)
    for k in ("w_in", "lru_w_r", "lru_w_i", "w_gate", "w_ffn_in"):
        shared[k] = np.ascontiguousarray(inp[k], dtype=np.float32)
    shared["w_mod"] = np.ascontiguousarray(
        np.asarray(inp["w_mod"], np.float32).reshape(DEPTH, NJ, 128, 96, 128).transpose(0, 3, 2, 1, 4))
    shared["w_out"] = np.ascontiguousarray(
        np.asarray(inp["w_out"], np.float32).reshape(DEPTH, NJ, 128, NJ, 128).transpose(0, 3, 2, 1, 4))
    shared["w_ffn_out"] = np.ascontiguousarray(
        np.asarray(inp["w_ffn_out"], np.float32).reshape(DEPTH, 44, 128, NJ, 128).transpose(0, 3, 2, 1, 4))
    shared["w_branch"] = np.ascontiguousarray(
        np.asarray(inp["w_branch"], np.float32).reshape(DEPTH, 3, 8, 128, NJ, 128).transpose(0, 4, 3, 1, 2, 5))
    maps = []
    for b in cores:
        cvec = np.stack([_col(inp["c"][b]), _col(inp["c_ctx"])], axis=-1)
        m = dict(shared)
        m["x"] = np.ascontiguousarray(inp["x"][b], dtype=np.float32)
        m["ctx"] = np.ascontiguousarray(inp["ctx"][b], dtype=np.float32)
        m["cvec"] = np.ascontiguousarray(cvec, dtype=np.float32)
        maps.append(m)
    return maps


def kernel(**inputs):
    n = 8
    nc = Builder(DEPTH).build()
    in_maps = make_in_maps(inputs, list(range(n)))
    res = run_bass_kernel_spmd(nc, in_maps, core_ids=list(range(n)))
    return np.stack([np.asarray(r["out"], dtype=np.float32) for r in res.results], axis=0)
```
